# Optimizing a Trainium2 kernel written in Bass

```python
import math
import jax, jax.numpy as jnp
from jax import lax
import numpy as np

D_MODEL = 2048
BATCH = 2
SEQ = 4096
DEPTH = 1

MOBA_HEADS = 8
MOBA_HEAD_DIM = 128
MOBA_WIDTH = MOBA_HEADS * MOBA_HEAD_DIM
MOBA_BLOCK = 256
MOBA_TOPK = 3
MOBA_Q_CHUNK = 32
DIFF_HEADS = 8
DIFF_QK_DIM = 64
DIFF_V_DIM = 2 * DIFF_QK_DIM
DIFF_QK_WIDTH = DIFF_HEADS * 2 * DIFF_QK_DIM
DIFF_WIDTH = DIFF_HEADS * DIFF_V_DIM
DIFF_Q_BLOCK = 128
ROPE_THETA = 10000.0
FFN_HIDDEN = ((8 * D_MODEL + 3 * 256 - 1) // (3 * 256)) * 256
LN_EPS = 1e-5
RMS_EPS = 1e-5
DEEPNORM_ALPHA = (2.0 * DEPTH) ** 0.25
DEEPNORM_BETA = (8.0 * DEPTH) ** -0.25
IN_SIZES = (MOBA_WIDTH, MOBA_WIDTH, MOBA_WIDTH,
            DIFF_QK_WIDTH, DIFF_QK_WIDTH, DIFF_WIDTH,
            D_MODEL, D_MODEL)
IN_COLS = sum(IN_SIZES)
IN_SPLITS = tuple(int(s) for s in np.cumsum(IN_SIZES)[:-1])

kernel_name = 'moba_diffattn_gated_deepnorm_block'


def _rope(t, pos):
    d = t.shape[-1]
    half = d // 2
    inv_freq = ROPE_THETA ** (-jnp.arange(half, dtype=jnp.float32) * 2.0 / d)
    ang = pos.astype(jnp.float32)[:, None] * inv_freq[None, :]
    shape = (pos.shape[0],) + (1,) * (t.ndim - 3) + (half,)
    cos = jnp.cos(ang).reshape(shape)
    sin = jnp.sin(ang).reshape(shape)
    tf = t.astype(jnp.float32)
    t1, t2 = tf[..., :half], tf[..., half:]
    return jnp.concatenate([t1 * cos - t2 * sin, t2 * cos + t1 * sin], axis=-1).astype(t.dtype)


def _layer_norm(t, g, b):
    tf = t.astype(jnp.float32)
    mu = jnp.mean(tf, axis=-1, keepdims=True)
    var = jnp.mean(jnp.square(tf - mu), axis=-1, keepdims=True)
    return ((tf - mu) * lax.rsqrt(var + LN_EPS) * g.astype(jnp.float32) + b.astype(jnp.float32)).astype(t.dtype)


def _moba_attention(q, k, v):
    B, S, H, Dh = q.shape
    nb = -(-S // MOBA_BLOCK)
    s_pad = nb * MOBA_BLOCK
    pad = ((0, 0), (0, s_pad - S), (0, 0), (0, 0))
    qh = jnp.pad(q, pad).astype(jnp.float32).transpose(0, 2, 1, 3)
    kb = jnp.pad(k, pad).astype(jnp.float32).transpose(0, 2, 1, 3).reshape(B, H, nb, MOBA_BLOCK, Dh)
    vb = jnp.pad(v, pad).astype(jnp.float32).transpose(0, 2, 1, 3).reshape(B, H, nb, MOBA_BLOCK, Dh)
    k_mean = jnp.mean(kb, axis=3)
    gate = jnp.einsum('bhsd,bhnd->bhsn', qh, k_mean)
    q_blk = jnp.arange(s_pad) // MOBA_BLOCK
    past = jnp.arange(nb)[None, :] < q_blk[:, None]
    gate = jnp.where(past[None, None], gate, -jnp.inf)
    topk = min(MOBA_TOPK, nb)
    _, sel = lax.top_k(gate, topk)
    sel_valid = jnp.arange(topk)[None, :] < q_blk[:, None]
    C = MOBA_Q_CHUNK
    n_chunk = s_pad // C
    q_c = qh.reshape(B, H, n_chunk, C, Dh).transpose(2, 0, 1, 3, 4)
    sel_c = sel.reshape(B, H, n_chunk, C, topk).transpose(2, 0, 1, 3, 4)
    valid_c = sel_valid.reshape(n_chunk, C, topk)
    scale = Dh ** -0.5
    bi = jnp.arange(B)[:, None, None, None]
    hi = jnp.arange(H)[None, :, None, None]
    key_off = jnp.arange(MOBA_BLOCK)

    def chunk_fn(args):
        c, qc, selc, validc = args
        kg = kb[bi, hi, selc]
        vg = vb[bi, hi, selc]
        s_sel = jnp.einsum('bhcd,bhcjkd->bhcjk', qc, kg) * scale
        s_sel = jnp.where(validc[None, None, :, :, None], s_sel, -jnp.inf).reshape(B, H, C, topk * MOBA_BLOCK)
        blk = (c * C) // MOBA_BLOCK
        k_own = lax.dynamic_index_in_dim(kb, blk, axis=2, keepdims=False)
        v_own = lax.dynamic_index_in_dim(vb, blk, axis=2, keepdims=False)
        s_own = jnp.einsum('bhcd,bhkd->bhck', qc, k_own) * scale
        q_off = (c * C) % MOBA_BLOCK + jnp.arange(C)
        s_own = jnp.where(key_off[None, :] <= q_off[:, None], s_own, -jnp.inf)
        p = jax.nn.softmax(jnp.concatenate([s_sel, s_own], axis=-1), axis=-1)
        p_sel = p[..., :topk * MOBA_BLOCK].reshape(B, H, C, topk, MOBA_BLOCK)
        p_own = p[..., topk * MOBA_BLOCK:]
        return (jnp.einsum('bhcjk,bhcjkd->bhcd', p_sel, vg)
                + jnp.einsum('bhck,bhkd->bhcd', p_own, v_own))

    out = lax.map(chunk_fn, (jnp.arange(n_chunk), q_c, sel_c, valid_c))
    out = out.transpose(1, 0, 3, 2, 4).reshape(B, s_pad, H * Dh)
    return out[:, :S]


def _diff_attention(q, k, v, lam, subln_w, lam_init):
    B, S, H, _, dq = q.shape
    dv = v.shape[-1]
    qh = q.astype(jnp.float32).transpose(0, 2, 3, 1, 4)
    kh = k.astype(jnp.float32).transpose(0, 2, 3, 1, 4)
    vh = v.astype(jnp.float32).transpose(0, 2, 1, 3)
    nq = S // DIFF_Q_BLOCK
    q_blocks = qh.reshape(B, H, 2, nq, DIFF_Q_BLOCK, dq).transpose(3, 0, 1, 2, 4, 5)
    kpos = jnp.arange(S)
    scale = dq ** -0.5

    def block_fn(args):
        i, qb = args
        s = jnp.einsum('bhmqd,bhmkd->bhmqk', qb, kh) * scale
        qpos = i * DIFF_Q_BLOCK + jnp.arange(DIFF_Q_BLOCK)
        s = jnp.where(kpos[None, :] <= qpos[:, None], s, -jnp.inf)
        p = jax.nn.softmax(s, axis=-1)
        a = p[:, :, 0] - lam * p[:, :, 1]
        return jnp.einsum('bhqk,bhkd->bhqd', a, vh)

    o = lax.map(block_fn, (jnp.arange(nq), q_blocks))
    o = o.transpose(1, 0, 3, 2, 4).reshape(B, S, H, dv)
    o = o * lax.rsqrt(jnp.mean(jnp.square(o), axis=-1, keepdims=True) + RMS_EPS)
    o = o * subln_w.astype(jnp.float32) * (1.0 - lam_init)
    return o.reshape(B, S, H * dv)


def _mixer(h, w_in, lambda_qk, subln_w, w_branch_a, w_branch_b, w_out, lam_init):
    B, S, _ = h.shape
    pos = jnp.arange(S)
    z = h @ w_in
    qa, ka, va, qb, kb, vb, ga, gb = jnp.split(z, IN_SPLITS, axis=-1)
    qa = _rope(qa.reshape(B, S, MOBA_HEADS, MOBA_HEAD_DIM), pos)
    ka = _rope(ka.reshape(B, S, MOBA_HEADS, MOBA_HEAD_DIM), pos)
    va = va.reshape(B, S, MOBA_HEADS, MOBA_HEAD_DIM)
    y_a = _moba_attention(qa, ka, va).astype(h.dtype)
    lq = lambda_qk.astype(jnp.float32)
    lam = jnp.exp(jnp.sum(lq[0] * lq[1])) - jnp.exp(jnp.sum(lq[2] * lq[3])) + lam_init
    qb = _rope(qb.reshape(B, S, DIFF_HEADS, 2, DIFF_QK_DIM), pos)
    kb = _rope(kb.reshape(B, S, DIFF_HEADS, 2, DIFF_QK_DIM), pos)
    vb = vb.reshape(B, S, DIFF_HEADS, DIFF_V_DIM)
    y_b = _diff_attention(qb, kb, vb, lam, subln_w, lam_init).astype(h.dtype)
    m = jax.nn.sigmoid(ga) * (y_a @ w_branch_a) + jax.nn.sigmoid(gb) * (y_b @ w_branch_b)
    return m @ w_out


def _swiglu(h, w_ffn_in, w_ffn_out):
    g, u = jnp.split(h @ w_ffn_in, 2, axis=-1)
    return (jax.nn.silu(g) * u) @ w_ffn_out


def setup_inputs(seed: int = 0) -> dict:
    key = jax.random.key(seed)
    ks = jax.random.split(key, 14)
    f32 = jnp.float32
    col_scale = np.ones((IN_COLS,), np.float32)
    starts = (0,) + IN_SPLITS
    for idx in (2, 5):
        col_scale[starts[idx]:starts[idx] + IN_SIZES[idx]] = DEEPNORM_BETA
    x = jax.random.normal(ks[0], (BATCH, SEQ, D_MODEL), f32)
    w_in = jax.random.normal(ks[1], (DEPTH, D_MODEL, IN_COLS), f32) * (D_MODEL ** -0.5) * jnp.asarray(col_scale)
    lambda_qk = 0.1 * jax.random.normal(ks[2], (DEPTH, 4, DIFF_QK_DIM), f32)
    diff_subln_w = 1.0 + 0.02 * jax.random.normal(ks[3], (DEPTH, DIFF_V_DIM), f32)
    w_branch_a = jax.random.normal(ks[4], (DEPTH, MOBA_WIDTH, D_MODEL), f32) * (MOBA_WIDTH ** -0.5) * DEEPNORM_BETA
    w_branch_b = jax.random.normal(ks[5], (DEPTH, DIFF_WIDTH, D_MODEL), f32) * (DIFF_WIDTH ** -0.5) * DEEPNORM_BETA
    w_out = jax.random.normal(ks[6], (DEPTH, D_MODEL, D_MODEL), f32) * (D_MODEL ** -0.5) * DEEPNORM_BETA
    ln1_g = 1.0 + 0.02 * jax.random.normal(ks[7], (DEPTH, D_MODEL), f32)
    ln1_b = 0.02 * jax.random.normal(ks[8], (DEPTH, D_MODEL), f32)
    w_ffn_in = jax.random.normal(ks[9], (DEPTH, D_MODEL, 2 * FFN_HIDDEN), f32) * (D_MODEL ** -0.5) * DEEPNORM_BETA
    w_ffn_out = jax.random.normal(ks[10], (DEPTH, FFN_HIDDEN, D_MODEL), f32) * (FFN_HIDDEN ** -0.5) * DEEPNORM_BETA
    ln2_g = 1.0 + 0.02 * jax.random.normal(ks[11], (DEPTH, D_MODEL), f32)
    ln2_b = 0.02 * jax.random.normal(ks[12], (DEPTH, D_MODEL), f32)
    return {'x': x, 'w_in': w_in, 'lambda_qk': lambda_qk, 'diff_subln_w': diff_subln_w,
            'w_branch_a': w_branch_a, 'w_branch_b': w_branch_b, 'w_out': w_out,
            'ln1_g': ln1_g, 'ln1_b': ln1_b, 'w_ffn_in': w_ffn_in, 'w_ffn_out': w_ffn_out,
            'ln2_g': ln2_g, 'ln2_b': ln2_b}


def reference(x, w_in, lambda_qk, diff_subln_w, w_branch_a, w_branch_b, w_out,
              ln1_g, ln1_b, w_ffn_in, w_ffn_out, ln2_g, ln2_b):
    h = x
    for l in range(DEPTH):
        lam_init = 0.8 - 0.6 * math.exp(-0.3 * l)
        mix = _mixer(h, w_in[l], lambda_qk[l], diff_subln_w[l], w_branch_a[l], w_branch_b[l], w_out[l], lam_init)
        h = _layer_norm(DEEPNORM_ALPHA * h + mix, ln1_g[l], ln1_b[l])
        h = _layer_norm(DEEPNORM_ALPHA * h + _swiglu(h, w_ffn_in[l], w_ffn_out[l]), ln2_g[l], ln2_b[l])
    return h
```

```python
import math
import numpy as np
import concourse.bass as bass
import concourse.mybir as mybir
from concourse.bass_utils import run_bass_kernel_spmd

F32 = mybir.dt.float32
BF16 = mybir.dt.bfloat16
AF = mybir.ActivationFunctionType
ALU = mybir.AluOpType
AX = mybir.AxisListType

NCORES = 8
D = 2048
T = 1024
NEG = -30000.0
ALPHA = 2.0 ** 0.25
LAM_INIT = 0.2
FF = 5632
DEBUG = False


class Buf:
    __slots__ = ("name", "writers", "readers", "sem", "count")

    def __init__(self, name):
        self.name = name
        self.writers = []
        self.readers = []
        self.sem = None
        self.count = 0


class Op:
    __slots__ = ("eng", "fn", "deps", "ms", "idx", "dma", "dsem", "dval", "inc")

    def __init__(self, eng, fn, dma=False):
        self.eng = eng
        self.fn = fn
        self.deps = []
        self.ms = False
        self.idx = None
        self.dma = dma
        self.dsem = None
        self.dval = 0
        self.inc = 16


ENGS = ("pe", "act", "dve", "pool", "sp")
ENGOBJ = {"pe": "tensor", "act": "scalar", "dve": "vector", "pool": "gpsimd", "sp": "sync"}


class Prog:
    def __init__(self, nc):
        self.nc = nc
        self.ops = {e: [] for e in ENGS}
        self.bufs = []
        self.esem = {e: nc.alloc_semaphore("e_" + e) for e in ENGS if e != "sp"}
        self.ecount = {e: 0 for e in ENGS}
        self.final = []

    def buf(self, name):
        b = Buf(name)
        self.bufs.append(b)
        return b

    def _track(self, op, reads, writes):
        deps = []
        for b in reads:
            deps.extend(b.writers)
        for b in writes:
            deps.extend(b.writers)
            deps.extend(b.readers)
        seen = set()
        for d in deps:
            if d is op or id(d) in seen:
                continue
            seen.add(id(d))
            if d.eng == "pe" and op.eng == "pe" and not d.dma:
                continue
            op.deps.append(d)
            if not d.dma:
                d.ms = True
        for b in reads:
            b.readers.append(op)
        for b in writes:
            b.writers = [op]
            b.readers = []

    def op(self, eng, fn, reads=(), writes=()):
        o = Op(eng, fn)
        self._track(o, reads, writes)
        self.ops[eng].append(o)
        return o

    def _own(self, o, owner):
        if owner.sem is None:
            owner.sem = self.nc.alloc_semaphore("d_" + owner.name)
        owner.count += 16
        o.dsem = owner.sem
        o.dval = owner.count

    def dma(self, q, out, in_, reads=(), writes=(), owner=None):
        o = Op(q, lambda e: e.dma_start(out=out, in_=in_), dma=True)
        self._own(o, owner if owner is not None else writes[0])
        self._track(o, reads, writes)
        self.ops[q].append(o)
        return o

    def cc(self, fn, reads=(), writes=()):
        o = Op("pool", fn, dma=True)
        self.ncc = getattr(self, "ncc", 0) + 1
        o.dsem = self.nc.alloc_semaphore(f"cc{self.ncc}")
        o.dval = 1
        o.inc = 1
        self._track(o, reads, writes)
        self.ops["pool"].append(o)
        return o

    def emit(self):
        nc = self.nc
        for e in ENGS:
            for o in self.ops[e]:
                if o.ms and not o.dma:
                    self.ecount[e] += 1
                    o.idx = self.ecount[e]
        dsems = {}
        for e in ENGS:
            for o in self.ops[e]:
                if o.dma:
                    k = id(o.dsem)
                    if k not in dsems or dsems[k][1] < o.dval:
                        dsems[k] = (o.dsem, o.dval)

        def run(e, eng):
            waited = {}
            for o in self.ops[e]:
                for d in o.deps:
                    if d.dma:
                        key, s, v = ("d", id(d.dsem)), d.dsem, d.dval
                    else:
                        key, s, v = ("e", d.eng), self.esem[d.eng], d.idx
                    if waited.get(key, 0) >= v:
                        continue
                    waited[key] = v
                    eng.wait_ge(s, v)
                inst = o.fn(eng)
                if o.dma:
                    inst.then_inc(o.dsem, o.inc)
                elif o.ms:
                    inst.then_inc(self.esem[e], 1)
            if e == "sp":
                for s, v in dsems.values():
                    eng.wait_ge(s, v)

        with nc.Block() as block:
            for e in ENGS:
                if not self.ops[e] and e != "sp":
                    continue
                getattr(block, ENGOBJ[e])(lambda eng, e=e: run(e, eng))
        self.ops = {e: [] for e in ENGS}
        for b in self.bufs:
            b.writers = []
            b.readers = []


class Scope:
    def __init__(self, nc):
        self.nc = nc
        self.cms = []

    def t(self, *a, **k):
        cm = self.nc.sbuf_tensor(*a, **k)
        self.cms.append(cm)
        return cm.__enter__()

    def __enter__(self):
        return self

    def __exit__(self, *exc):
        for cm in reversed(self.cms):
            cm.__exit__(None, None, None)
        return False


def build_nc():
    nc = bass.Bass("TRN2", target_bir_lowering=False)

    def din(name, shape, dt=F32):
        return nc.dram_tensor(name, list(shape), dt, kind="ExternalInput").ap()

    xT = din("xT", [D, T])
    xtok = din("xtok", [T, D])
    w_in = din("w_in", [D, 10240])
    w_ba = din("w_ba", [1024, D])
    w_bb = din("w_bb", [1024, D])
    w_out = din("w_out", [D, D])
    w_f1 = din("w_f1", [D, 2 * FF])
    w_f2 = din("w_f2", [FF, D])
    lnp = din("lnp", [4, D])
    lam_in = din("lam", [1, 256])
    subln = din("subln", [1, 128])
    ropeA = din("ropeA", [T, 256])
    ropeB = din("ropeB", [T, 128])
    mtab = din("mtab", [128, 3 * 160])
    slotb = din("slotb", [128, 8])
    cst = din("cst", [128, 256])
    esel_in = din("esel", [128, 2560])
    y = nc.dram_tensor("y", [T, D], F32, kind="ExternalOutput").ap()
    if DEBUG:
        dbg = nc.dram_tensor("dbg", [6, 128, 16 * 1024], BF16, kind="ExternalOutput").ap()
        dbgx1 = nc.dram_tensor("dbgx1", [T, D], F32, kind="ExternalOutput").ap()

    bK = [nc.dram_tensor(f"bK{c}", [512, 1024], BF16).ap() for c in range(4)]
    gK = [nc.dram_tensor(f"gK{c}", [4 * 512, 1024], BF16).ap() for c in range(4)]
    bV = [nc.dram_tensor(f"bV{c}", [1024, 512], BF16).ap() for c in range(4)]
    gV = [nc.dram_tensor(f"gV{c}", [4 * 1024, 512], BF16).ap() for c in range(4)]
    x1d = nc.dram_tensor("x1d", [T, D], F32)
    x1d_ap = x1d.ap()

    P = Prog(nc)
    sb = nc.alloc_sbuf_tensor

    banks = [nc.alloc_psum_tensor(f"bank{i}", [128, 512], F32) for i in range(8)]
    bankb = [P.buf(f"bank{i}") for i in range(8)]

    class Ring:
        def __init__(self, ids):
            self.ids = ids
            self.i = 0

        def next(self):
            k = self.ids[self.i % len(self.ids)]
            self.i += 1
            return banks[k], bankb[k]

    ident = sb("ident_sb", [128, 128], BF16)
    tri = sb("tri_sb", [128, 128], BF16)
    esel = sb("esel_sb", [128, 2560], BF16)
    b_const = P.buf("const")
    P.dma("pool", tri[:], cst[:, 0:128], writes=[b_const])
    P.dma("pool", ident[:], cst[:, 128:256], writes=[b_const])
    P.dma("pool", esel[:], esel_in, writes=[b_const])

    qaT_cm = nc.sbuf_tensor("qaT", [128, 8, T], BF16, side="right")
    qbT_cm = nc.sbuf_tensor("qbT", [128, 8, T], BF16, side="right")
    qaT = qaT_cm.__enter__()
    qbT = qbT_cm.__enter__()
    b_qaT = P.buf("qaT")
    b_qbT = P.buf("qbT")

    sc = Scope(nc)
    xT_sb = sc.t("xT_sb", [128, 16, T], BF16)
    wq = sc.t("wq", [128, 2, 16, 512], BF16)
    kaT = sc.t("kaT", [128, 8, T], BF16)
    kbT = sc.t("kbT", [128, 8, T], BF16)
    va_sb = sc.t("va", [128, 8, 1024], BF16)
    vb_sb = sc.t("vb", [128, 8, 1024], BF16)
    ropeA_sb = sc.t("ropeA_sb", [128, 8, 256], F32)
    ropeB_sb = sc.t("ropeB_sb", [128, 8, 128], F32)
    zf = sc.t("zf", [128, 3, 512], F32)
    tmp1 = sc.t("tmp1", [128, 2, 512], F32)
    tmp2 = sc.t("tmp2", [128, 2, 512], F32)
    zr = sc.t("zr", [128, 3, 512], BF16)
    with sc:
        b_xT = P.buf("xT")
        for q4 in range(4):
            P.dma("pool", xT_sb[:, q4 * 4:(q4 + 1) * 4, :],
                  xT[q4 * 512:(q4 + 1) * 512, :].rearrange("(k p) t -> p k t", p=128), writes=[b_xT])
        b_rope = P.buf("rope")
        P.dma("sp", ropeA_sb[:], ropeA.rearrange("(t p) c -> p t c", p=128), writes=[b_rope])
        P.dma("sp", ropeB_sb[:], ropeB.rearrange("(t p) c -> p t c", p=128), writes=[b_rope])
        b_w = [P.buf("wq0"), P.buf("wq1")]
        b_zf = [P.buf(f"zf{i}") for i in range(3)]
        b_t1 = [P.buf(f"t1{i}") for i in range(2)]
        b_t2 = [P.buf(f"t2{i}") for i in range(2)]
        b_zr = [P.buf(f"zr{i}") for i in range(3)]
        b_kT = [P.buf(f"kT{c}") for c in range(4)]
        b_vS = [P.buf(f"vS{c}") for c in range(4)]
        b_bK = [P.buf(f"bK{c}") for c in range(4)]
        b_bV = [P.buf(f"bV{c}") for c in range(4)]
        b_gK = [P.buf(f"gK{c}") for c in range(4)]
        b_gV = [P.buf(f"gV{c}") for c in range(4)]
        RG = [[0, 1, 2, 3], [4, 5, 6, 7]]
        zring = Ring([0, 1, 2, 3])
        tring = Ring([4, 5])
        kinds = ["qa", "qa", "ka", "ka", "va", "va", "qb", "qb", "kb", "kb", "vb", "vb"]
        dests = {"qa": (qaT, None), "ka": (kaT, 0), "qb": (qbT, None), "kb": (kbT, 2)}
        pending = []
        cnt = 0
        order = [2, 3, 8, 9, 4, 5, 10, 11, 0, 1, 6, 7]

        def flush(keep):
            while len(pending) > keep:
                pending.pop(0)()

        def load_w(idx):
            s = idx % 2
            n = order[idx]
            P.dma("pool", wq[:, s], w_in[:, n * 512:(n + 1) * 512].rearrange("(k p) c -> p k c", p=128),
                  writes=[b_w[s]])

        order = [2, 3, 8, 9, 4, 5, 10, 11, 0, 1, 6, 7]
        load_w(0)
        for idx in range(12):
            n = order[idx]
            if idx + 1 < 12:
                load_w(idx + 1)
            s = idx % 2
            kind = kinds[n]
            for t in range(8):
                ps, bps = zring.next()

                def mmf(e, ps=ps, s=s, t=t):
                    for k in range(16):
                        i = e.matmul(ps[:], lhsT=xT_sb[:, k, t * 128:(t + 1) * 128], rhs=wq[:, s, k, :],
                                     start=(k == 0), stop=(k == 15))
                    return i
                P.op("pe", mmf, reads=[b_xT, b_w[s]], writes=[bps])
                if kind in ("va", "vb"):
                    dst = va_sb if kind == "va" else vb_sb
                    bd = b_vS[(0 if kind == "va" else 2) + n % 2]
                    c0 = (n % 2) * 512
                    P.op("act", lambda e, ps=ps, dst=dst, t=t, c0=c0: e.copy(out=dst[:, t, c0:c0 + 512], in_=ps[:]),
                         reads=[bps], writes=[bd])
                    flush(1)
                    continue
                i3 = cnt % 3
                i2 = cnt % 2
                cnt += 1
                P.op("act", lambda e, ps=ps, i3=i3: e.copy(out=zf[:, i3, :], in_=ps[:]), reads=[bps], writes=[b_zf[i3]])
                if kind in ("qa", "ka"):
                    nh, hd, tab, tw = 4, 128, ropeA_sb, 128
                else:
                    nh, hd, tab, tw = 8, 64, ropeB_sb, 64
                hf = hd // 2
                z3 = zf[:, i3, :].rearrange("p (h d) -> p h d", d=hd)
                a3 = tmp1[:, i2, :].rearrange("p (h d) -> p h d", d=hd)
                c3 = tmp2[:, i2, :].rearrange("p (h d) -> p h d", d=hd)
                cosb = tab[:, t, 0:tw].unsqueeze(1).broadcast_to([128, nh, hd])
                sin_lo = tab[:, t, tw:tw + hf].unsqueeze(1).broadcast_to([128, nh, hf])
                sin_hi = tab[:, t, tw + hf:tw + hd].unsqueeze(1).broadcast_to([128, nh, hf])
                P.op("dve", lambda e, a3=a3, z3=z3, cosb=cosb: e.tensor_tensor(out=a3, in0=z3, in1=cosb, op=ALU.mult),
                     reads=[b_zf[i3], b_rope], writes=[b_t1[i2]])

                def r2(e, c3=c3, z3=z3, sin_lo=sin_lo, sin_hi=sin_hi, hf=hf, hd=hd):
                    e.tensor_tensor(out=c3[:, :, 0:hf], in0=z3[:, :, hf:hd], in1=sin_lo, op=ALU.mult)
                    return e.tensor_tensor(out=c3[:, :, hf:hd], in0=z3[:, :, 0:hf], in1=sin_hi, op=ALU.mult)
                P.op("dve", r2, reads=[b_zf[i3], b_rope], writes=[b_t2[i2]])
                P.op("dve", lambda e, i2=i2, i3=i3: e.tensor_tensor(out=zr[:, i3, :], in0=tmp1[:, i2, :], in1=tmp2[:, i2, :], op=ALU.add),
                     reads=[b_t1[i2], b_t2[i2]], writes=[b_zr[i3]])
                dst, cb = dests[kind]
                if cb is None:
                    bd = b_qaT if kind == "qa" else b_qbT
                else:
                    bd = b_kT[cb + n % 2]
                h0 = (n % 2) * 4

                def trans(i3=i3, dst=dst, bd=bd, h0=h0, t=t):
                    pt, bpt = tring.next()
                    ptb = pt[:].bitcast(BF16)

                    def tf(e):
                        for c in range(4):
                            i = e.transpose(out=ptb[:, c * 128:(c + 1) * 128], in_=zr[:, i3, c * 128:(c + 1) * 128], identity=ident[:])
                        return i
                    P.op("pe", tf, reads=[b_zr[i3], b_const], writes=[bpt])
                    P.op("act", lambda e: e.copy(out=dst[:, h0:h0 + 4, t * 128:(t + 1) * 128],
                                                 in_=ptb[:, 0:512].rearrange("p (c q) -> p c q", q=128)),
                         reads=[bpt], writes=[bd])
                pending.append(trans)
                flush(1)
            if kind in ("ka", "kb"):
                flush(0)
                c = (0 if kind == "ka" else 2) + n % 2
                src = (kaT if kind == "ka" else kbT)[:, (n % 2) * 4:(n % 2) * 4 + 4, :]
                P.dma("sp", bK[c].rearrange("(h p) t -> p h t", p=128), src, reads=[b_kT[c]], writes=[b_bK[c]])
                if kind == "ka":
                    P.cc(lambda e, c=c: e.collective_compute("AllGather", ALU.bypass, replica_groups=RG, ins=[bK[c]], outs=[gK[c]]),
                         reads=[b_bK[c]], writes=[b_gK[c]])
            if kind in ("va", "vb"):
                c = (0 if kind == "va" else 2) + n % 2
                src = (va_sb if kind == "va" else vb_sb)[:, :, (n % 2) * 512:(n % 2) * 512 + 512]
                P.dma("sp", bV[c].rearrange("(t p) c -> p t c", p=128), src, reads=[b_vS[c]], writes=[b_bV[c]])
                if kind == "va":
                    P.cc(lambda e, c=c: e.collective_compute("AllGather", ALU.bypass, replica_groups=RG, ins=[bV[c]], outs=[gV[c]]),
                         reads=[b_bV[c]], writes=[b_gV[c]])
        flush(0)
        if DEBUG:
            b_dbg = P.buf("dbg")
            P.dma("sp", dbg[0, :, 0:8192], qaT[:].rearrange("p h t -> p (h t)"), reads=[b_qaT], writes=[b_dbg])
            P.dma("sp", dbg[1, :, 0:8192], qbT[:].rearrange("p h t -> p (h t)"), reads=[b_qbT], writes=[b_dbg])
        P.emit()

    yaT_cm = nc.sbuf_tensor("yaT", [128, 8, T], BF16)
    ybT_cm = nc.sbuf_tensor("ybT", [128, 8, T], BF16)
    yaT = yaT_cm.__enter__()
    ybT = ybT_cm.__enter__()
    b_yaT, b_ybT = P.buf("yaT"), P.buf("ybT")
    xT2_cm = nc.sbuf_tensor("xT_sb2", [128, 16, T], BF16)
    xT2 = xT2_cm.__enter__()
    b_xT2 = P.buf("xT2")
    sc = Scope(nc)
    Kp = sc.t("Kp", [128, 2, 5, 1024], BF16)
    K1 = sc.t("K1", [128, 2, 5, 1024], BF16)
    Vp = sc.t("Vp", [128, 2, 5, 8, 258], BF16)
    pt = sc.t("pt", [128, 6, 512], BF16)
    mtab_sb = sc.t("mtab_sb", [128, 3, 8, 20], F32)
    slotb_sb = sc.t("slotb_sb", [128, 8], F32)
    lq = sc.t("lq", [128, 256], F32)
    sm = sc.t("sm", [128, 64], F32)
    wrow = sc.t("wrow", [128, 128], F32)
    kmT = sc.t("kmT", [128, 2, 20], BF16)
    ksum = sc.t("ksum", [128, 2, 20], F32)
    gate = sc.t("gate", [128, 2, 8, 20], F32)
    gsel = sc.t("gsel", [128, 2, 8, 20], F32)
    m8 = sc.t("m8", [128, 2, 8, 8], F32)
    mb = sc.t("mb", [128, 2, 8, 20], BF16)
    mbT = sc.t("mbT", [128, 2, T], BF16)
    rec = sc.t("rec", [128, 2, 8], F32)
    ssq = sc.t("ssq", [128, 2, 4], F32)
    of_ = sc.t("of", [128, 2, 4, 128], F32)
    junk = sc.t("junk", [128, 128], F32)
    ytok = sc.t("ytok", [128, 2, 4, 128], BF16)
    with sc:
        b_mtab, b_slotb, b_lq, b_sm, b_wrow = P.buf("mtab"), P.buf("slotb"), P.buf("lq"), P.buf("sm"), P.buf("wrow")
        P.dma("sp", mtab_sb[:].rearrange("p a t i -> p (a t i)"), mtab, writes=[b_mtab])
        P.dma("sp", slotb_sb[:], slotb, writes=[b_slotb])
        P.dma("sp", lq[:], lam_in.partition_broadcast(128).rearrange("p o c -> p (o c)"), writes=[b_lq])
        P.dma("sp", wrow[:], subln.partition_broadcast(128).rearrange("p o c -> p (o c)"), writes=[b_wrow])
        P.op("dve", lambda e: e.scalar_tensor_tensor(out=junk[:, 0:64], in0=lq[:, 0:64], scalar=1.0, in1=lq[:, 64:128], op0=ALU.mult, op1=ALU.mult, accum_out=sm[:, 0:1]), reads=[b_lq], writes=[b_sm])
        P.op("dve", lambda e: e.scalar_tensor_tensor(out=junk[:, 64:128], in0=lq[:, 128:192], scalar=1.0, in1=lq[:, 192:256], op0=ALU.mult, op1=ALU.mult, accum_out=sm[:, 1:2]), reads=[b_lq, b_sm], writes=[b_sm])
        P.op("act", lambda e: e.activation(out=sm[:, 2:4], in_=sm[:, 0:2], func=AF.Exp), reads=[b_sm], writes=[b_sm])
        P.op("dve", lambda e: e.tensor_tensor(out=sm[:, 4:5], in0=sm[:, 3:4], in1=sm[:, 2:3], op=ALU.subtract), reads=[b_sm], writes=[b_sm])
        P.op("dve", lambda e: e.tensor_scalar(out=sm[:, 4:5], in0=sm[:, 4:5], scalar1=-LAM_INIT, scalar2=None, op0=ALU.add), reads=[b_sm], writes=[b_sm])
        P.op("dve", lambda e: e.tensor_scalar(out=wrow[:], in0=wrow[:], scalar1=1.0 - LAM_INIT, scalar2=None, op0=ALU.mult), reads=[b_wrow], writes=[b_wrow])
        b_K = [P.buf("K0s"), P.buf("K1s")]
        b_V = [P.buf("V0s"), P.buf("V1s")]
        late_cc = {1: ("K", 2), 2: ("V", 2), 4: ("K", 3), 5: ("V", 3)}

        def issue_cc(kv, c):
            if kv == "K":
                P.cc(lambda e: e.collective_compute("AllGather", ALU.bypass, replica_groups=RG, ins=[bK[c]], outs=[gK[c]]),
                     reads=[b_bK[c]], writes=[b_gK[c]])
            else:
                P.cc(lambda e: e.collective_compute("AllGather", ALU.bypass, replica_groups=RG, ins=[bV[c]], outs=[gV[c]]),
                     reads=[b_bV[c]], writes=[b_gV[c]])
        P.op("pool", lambda e: e.memset(K1[:], 0.0), writes=b_K)
        P.op("pool", lambda e: e.memset(Vp[:], 1.0), writes=b_V)
        b_pt = [P.buf(f"pt{i}") for i in range(6)]
        b_small = [P.buf("small0"), P.buf("small1")]
        b_rec = P.buf("rec")
        b_mbT = [P.buf("mbT0"), P.buf("mbT1")]
        P.op("dve", lambda e: e.memset(mbT[:], 0.0), writes=b_mbT)
        b_fin = [P.buf("fin0"), P.buf("fin1")]
        sring = Ring([0, 1, 2, 3])
        def load_K(hidx):
            mx, h, ks = hidx // 8, hidx % 8, hidx % 2
            c = mx * 2 + h // 4
            r0 = (h % 4) * 128
            gv = gK[c].rearrange("(r x) t -> r x t", r=4)
            if mx == 0:
                P.dma("sp", Kp[:, ks, 0:4, :], gv[:, r0:r0 + 128, :].rearrange("r p t -> p r t"), reads=[b_gK[c]], writes=[b_K[ks]])
                P.dma("sp", Kp[:, ks, 4, :], bK[c][r0:r0 + 128, :], writes=[b_K[ks]])
            else:
                if hidx in (8, 9):
                    P.op("pool", lambda e, ks=ks: e.memset(Kp[64:128, ks], 0.0), writes=[b_K[ks]])
                for (Kt, lo) in ((Kp, 0), (K1, 64)):
                    P.dma("sp", Kt[lo:lo + 64, ks, 0:4, :], gv[:, r0 + lo:r0 + lo + 64, :].rearrange("r p t -> p r t"), reads=[b_gK[c]], writes=[b_K[ks]])
                    P.dma("sp", Kt[lo:lo + 64, ks, 4, :], bK[c][r0 + lo:r0 + lo + 64, :], writes=[b_K[ks]])

        def load_V(pi):
            mx, hp, vs = pi // 4, pi % 4, pi % 2
            c = mx * 2 + hp // 2
            c0 = (hp % 2) * 256
            gv = gV[c].rearrange("(r x) d -> r x d", r=4)
            for rr in range(4):
                P.dma("sp", Vp[:, vs, rr, :, 1:257], gv[rr, :, c0:c0 + 256].rearrange("(c p) d -> p c d", p=128), reads=[b_gV[c]], writes=[b_V[vs]])
            P.dma("sp", Vp[:, vs, 4, :, 1:257], bV[c][:, c0:c0 + 256].rearrange("(c p) d -> p c d", p=128), writes=[b_V[vs]])

        def kcol(kb):
            return (kb, 0) if kb < 4 else (7 - kb, 1)

        load_K(0)
        load_V(0)
        ptc = [0]
        ocnt = [0]
        fin_pending = []

        def flush_fin():
            while fin_pending:
                fin_pending.pop(0)()
        gate_ps = {}

        def gate_stage(h, stg):
            g = h % 2
            slot = h % 2
            bs = b_small[g]
            if stg == 0:
                kv = Kp[:, slot, 0:4, :].rearrange("p r (b k) -> p r b k", k=256)
                P.op("dve", lambda e: e.tensor_reduce(out=ksum[:, g, 0:16].rearrange("p (r b) -> p r b", b=4), in_=kv, axis=AX.X, op=ALU.add),
                     reads=[b_K[slot]], writes=[bs])
                kv2 = Kp[:, slot, 4, :].rearrange("p (b k) -> p b k", k=256)
                P.op("dve", lambda e: e.tensor_reduce(out=ksum[:, g, 16:20], in_=kv2, axis=AX.X, op=ALU.add),
                     reads=[b_K[slot], bs], writes=[bs])
                P.op("dve", lambda e: e.tensor_scalar(out=kmT[:, g, :], in0=ksum[:, g, :], scalar1=1.0 / 256.0, scalar2=None, op0=ALU.mult),
                     reads=[bs], writes=[bs])
            elif stg == 1:
                gp, bgp = sring.next()

                def gf(e):
                    for t in range(8):
                        i = e.matmul(gp[:, t * 20:(t + 1) * 20], lhsT=qaT[:, h, t * 128:(t + 1) * 128], rhs=kmT[:, g, :], start=True, stop=True)
                    return i
                P.op("pe", gf, reads=[b_qaT, bs], writes=[bgp])
                P.op("dve", lambda e: e.tensor_tensor(out=gate[:, g].rearrange("p t i -> p (t i)"), in0=gp[:, 0:160],
                                                      in1=mtab_sb[:, 0].rearrange("p t i -> p (t i)"), op=ALU.add),
                     reads=[bgp, b_mtab], writes=[bs])

                def selop1(e):
                    for t in range(8):
                        i = e.max(out=m8[:, g, t, :], in_=gate[:, g, t, :])
                    return i

                def selop2(e):
                    for t in range(8):
                        i = e.tensor_scalar(out=gsel[:, g, t, :], in0=gate[:, g, t, :], scalar1=m8[:, g, t, 2:3], scalar2=None, op0=ALU.is_ge)
                    return i
                P.op("dve", selop1, reads=[bs], writes=[bs])
                P.op("dve", selop2, reads=[bs], writes=[bs])
                P.op("dve", lambda e: e.tensor_tensor(out=gsel[:, g], in0=gsel[:, g], in1=mtab_sb[:, 1], op=ALU.mult), reads=[bs, b_mtab], writes=[bs])
                P.op("dve", lambda e: e.tensor_tensor(out=gsel[:, g], in0=gsel[:, g], in1=mtab_sb[:, 2], op=ALU.add), reads=[bs, b_mtab], writes=[bs])
                P.op("dve", lambda e: e.tensor_scalar(out=mb[:, g], in0=gsel[:, g], scalar1=-NEG, scalar2=NEG, op0=ALU.mult, op1=ALU.add),
                     reads=[bs], writes=[bs])
            else:
                tp, btp = sring.next()
                tpb = tp[:].bitcast(BF16)

                def tf(e):
                    for t in range(8):
                        i = e.transpose(out=tpb[0:20, t * 128:(t + 1) * 128], in_=mb[:, g, t, :], identity=ident[:])
                    return i
                P.op("pe", tf, reads=[bs, b_const], writes=[btp])
                P.op("act", lambda e: e.copy(out=mbT[0:20, g, :], in_=tpb[0:20, 0:1024]), reads=[btp], writes=[b_mbT[g]])

        for pi in range(8):
            mx, hp = pi // 4, pi % 4
            vslot = pi % 2
            if pi + 1 < 8:
                load_V(pi + 1)
            for hh in range(2):
                h = hp * 2 + hh
                hidx = pi * 2 + hh
                slot = hidx % 2
                if hidx + 1 < 16:
                    load_K(hidx + 1)
                if hidx in late_cc:
                    issue_cc(*late_cc[hidx])
                if hidx == 12:
                    for q4 in range(4):
                        P.dma("pool", xT2[:, q4 * 4:(q4 + 1) * 4, :],
                              xT[q4 * 512:(q4 + 1) * 512, :].rearrange("(k p) t -> p k t", p=128), writes=[b_xT2])
                qT = qaT if mx == 0 else qbT
                bq = b_qaT if mx == 0 else b_qbT
                nsub = 1 if mx == 0 else 2
                scale = (128.0 ** -0.5) if mx == 0 else 0.125
                if mx == 0 and h == 0:
                    for stg in range(3):
                        gate_stage(0, stg)
                gpar = h % 2
                for seg in range(2):
                    q0 = seg * 512
                    offk = [0, 1, 2] if seg == 0 else [0, 1, 2, 3, 4, 5, 6]
                    steps = []
                    for kb in offk:
                        rr, hf = kcol(kb)
                        sbi = None
                        if seg == 0:
                            sbi = kb
                        elif kb >= 4:
                            sbi = 3 + kb - 4
                        for c in range(4):
                            steps.append((rr, hf * 512 + c * 128, hf * 4 + c, c, False, rr * 4 + hf * 2 + c // 2, sbi))
                    for c in range(4):
                        steps.append((4, seg * 512 + c * 128, seg * 4 + c, c, True, 16 + seg * 2 + c // 2, None))
                    if nsub == 1:
                        ob0 = 4 + 2 * (ocnt[0] % 2)
                        ocnt[0] += 1
                        obanks = [(banks[ob0], bankb[ob0]), (banks[ob0 + 1], bankb[ob0 + 1])]
                    else:
                        obanks = [(banks[4], bankb[4]), (banks[5], bankb[5]), (banks[6], bankb[6]), (banks[7], bankb[7])]
                    first_pv = [True]
                    prev = [None]

                    def do_pv(st, ptis, slot=vslot, hh=hh, nsub=nsub, obanks=obanks, first_pv=first_pv):
                        kblk, koff, vch, c, diag, ei, sbi = st
                        r0 = c if diag else 0
                        vs = Vp[:, slot, kblk, vch, 0:129] if hh == 0 else Vp[:, slot, kblk, vch, 129:258]
                        fp = first_pv[0]
                        first_pv[0] = False

                        def pvf(e):
                            for m in range(nsub):
                                for r in range(r0, 4):
                                    ob = obanks[m * 2 + r // 2][0]
                                    i = e.matmul(ob[:, (r % 2) * 256:(r % 2) * 256 + 129], lhsT=pt[:, ptis[m], r * 128:(r + 1) * 128],
                                                 rhs=vs, start=(fp and r % 2 == 0), stop=True, skip_group_check=True)
                            return i
                        P.op("pe", pvf, reads=[b_pt[i] for i in ptis] + [b_V[slot]], writes=[ob[1] for ob in obanks])

                    for si, st in enumerate(steps):
                        kblk, koff, vch, c, diag, ei, sbi = st
                        n0 = c * 128 if diag else 0
                        if si == 3:
                            flush_fin()
                        if mx == 0 and h + 1 < 8 and seg == 1 and si in (6, 14, 22):
                            gate_stage(h + 1, (6, 14, 22).index(si))
                        ptis = []
                        for m in range(nsub):
                            sp_, bsp = sring.next()
                            kt = (Kp if m == 0 else K1)[:, slot, kblk, koff:koff + 128]

                            def qk(e, sp_=sp_, kt=kt, n0=n0, diag=diag, ei=ei, qT=qT, h=h, q0=q0, mx=mx, c=c, gpar=gpar):
                                last_plain = (mx == 1 and not diag)
                                i = e.matmul(sp_[:, n0:512], lhsT=kt, rhs=qT[:, h, q0 + n0:q0 + 512], start=True, stop=last_plain)
                                if mx == 0:
                                    i = e.matmul(sp_[:, n0:512], lhsT=esel[:, ei * 128:(ei + 1) * 128], rhs=mbT[:, gpar, q0 + n0:q0 + 512],
                                                 start=False, stop=(not diag))
                                if diag:
                                    i = e.matmul(sp_[:, n0:n0 + 128], lhsT=ident[:], rhs=tri[:], start=False, stop=True)
                                return i
                            P.op("pe", qk, reads=[b_K[slot], bq, b_const] + ([b_mbT[gpar]] if mx == 0 else []), writes=[bsp])
                            pi_ = ptc[0] % 6
                            ptc[0] += 1
                            ptis.append(pi_)
                            if sbi is not None and mx == 1:
                                P.op("act", lambda e, sp_=sp_, pi_=pi_, n0=n0, sbi=sbi, scale=scale: e.activation(
                                    out=pt[:, pi_, n0:512], in_=sp_[:, n0:512], func=AF.Exp, bias=slotb_sb[:, sbi:sbi + 1], scale=scale),
                                    reads=[bsp, b_slotb], writes=[b_pt[pi_]])
                            else:
                                P.op("act", lambda e, sp_=sp_, pi_=pi_, n0=n0, scale=scale: e.activation(
                                    out=pt[:, pi_, n0:512], in_=sp_[:, n0:512], func=AF.Exp, scale=scale),
                                    reads=[bsp], writes=[b_pt[pi_]])
                        if prev[0] is not None:
                            do_pv(*prev[0])
                        prev[0] = (st, ptis)
                    do_pv(*prev[0])
                    fs = seg
                    bf_ = b_fin[fs]
                    if mx == 0:
                        sc = 0 if hh == 0 else 128
                        d0 = 1 if hh == 0 else 0
                        ob0, ob1 = obanks[0][0], obanks[1][0]
                        for r in range(4):
                            ob = (ob0, ob1)[r // 2]
                            o0 = (r % 2) * 256
                            P.op("dve", lambda e, ob=ob, o0=o0, r=r, sc=sc, fs=fs: e.reciprocal(out=rec[:, fs, r:r + 1], in_=ob[:, o0 + sc:o0 + sc + 1]),
                                 reads=[obanks[r // 2][1]], writes=[b_rec])
                            P.op("dve", lambda e, ob=ob, o0=o0, r=r, d0=d0, fs=fs: e.tensor_scalar(out=ytok[:, fs, r, :], in0=ob[:, o0 + d0:o0 + d0 + 128],
                                                                                           scalar1=rec[:, fs, r:r + 1], scalar2=None, op0=ALU.mult),
                                 reads=[obanks[r // 2][1], b_rec], writes=[bf_])
                    else:
                        sc = 0 if hh == 0 else 128
                        d0 = 1 if hh == 0 else 0
                        for r in range(4):
                            oA = obanks[0 + r // 2][0]
                            oB = obanks[2 + r // 2][0]
                            o0 = (r % 2) * 256
                            rd_ = [obanks[0 + r // 2][1], obanks[2 + r // 2][1]]

                            def f1(e, oA=oA, oB=oB, o0=o0, r=r, sc=sc, fs=fs):
                                e.reciprocal(out=rec[:, fs, r:r + 1], in_=oA[:, o0 + sc:o0 + sc + 1])
                                return e.reciprocal(out=rec[:, fs, 4 + r:5 + r], in_=oB[:, o0 + sc:o0 + sc + 1])
                            P.op("dve", f1, reads=rd_, writes=[b_rec])
                            P.op("dve", lambda e, r=r, fs=fs: e.tensor_scalar(out=rec[:, fs, 4 + r:5 + r], in0=rec[:, fs, 4 + r:5 + r], scalar1=sm[:, 4:5],
                                                                       scalar2=None, op0=ALU.mult), reads=[b_rec, b_sm], writes=[b_rec])
                            P.op("dve", lambda e, oA=oA, o0=o0, r=r, d0=d0, fs=fs: e.tensor_scalar(out=of_[:, fs, r, :], in0=oA[:, o0 + d0:o0 + d0 + 128],
                                                                                           scalar1=rec[:, fs, r:r + 1], scalar2=None, op0=ALU.mult),
                                 reads=rd_ + [b_rec], writes=[bf_])
                            P.op("dve", lambda e, oB=oB, o0=o0, r=r, d0=d0, fs=fs: e.scalar_tensor_tensor(out=of_[:, fs, r, :], in0=oB[:, o0 + d0:o0 + d0 + 128],
                                                                                                  scalar=rec[:, fs, 4 + r:5 + r], in1=of_[:, fs, r, :],
                                                                                                  op0=ALU.mult, op1=ALU.add),
                                 reads=rd_ + [b_rec, bf_], writes=[bf_])
                            P.op("dve", lambda e, r=r, fs=fs: e.scalar_tensor_tensor(out=junk[:], in0=of_[:, fs, r, :], scalar=1.0, in1=of_[:, fs, r, :], op0=ALU.mult, op1=ALU.mult, accum_out=ssq[:, fs, r:r + 1]),
                                 reads=[bf_], writes=[b_rec])
                        P.op("dve", lambda e, fs=fs: e.tensor_scalar(out=ssq[:, fs, :], in0=ssq[:, fs, :], scalar1=1.0 / 128.0, scalar2=1e-5, op0=ALU.mult, op1=ALU.add),
                             reads=[b_rec], writes=[b_rec])
                        P.op("act", lambda e, fs=fs: e.activation(out=ssq[:, fs, :], in_=ssq[:, fs, :], func=AF.Ln), reads=[b_rec], writes=[b_rec])
                        P.op("act", lambda e, fs=fs: e.activation(out=ssq[:, fs, :], in_=ssq[:, fs, :], func=AF.Exp, scale=-0.5), reads=[b_rec], writes=[b_rec])
                        for r in range(4):
                            P.op("dve", lambda e, r=r, fs=fs: e.scalar_tensor_tensor(out=ytok[:, fs, r, :], in0=of_[:, fs, r, :], scalar=ssq[:, fs, r:r + 1], in1=wrow[:],
                                                                              op0=ALU.mult, op1=ALU.mult),
                                 reads=[bf_, b_rec, b_wrow], writes=[bf_])
                    def fin_t(fs=fs, bf_=bf_, mx=mx, h=h, q0=q0):
                        tp, btp = sring.next()
                        tpb = tp[:].bitcast(BF16)

                        def tf2(e):
                            for r in range(4):
                                i = e.transpose(out=tpb[:, r * 128:(r + 1) * 128], in_=ytok[:, fs, r, :], identity=ident[:])
                            return i
                        P.op("pe", tf2, reads=[bf_, b_const], writes=[btp])
                        yT, byT = (yaT, b_yaT) if mx == 0 else (ybT, b_ybT)
                        P.op("act", lambda e: e.copy(out=yT[:, h, q0:q0 + 512], in_=tpb[:, 0:512]), reads=[btp], writes=[byT])
                    fin_pending.append(fin_t)
        flush_fin()
        if DEBUG:
            P.dma("sp", dbg[2, :, 0:8192], yaT[:].rearrange("p h t -> p (h t)"), reads=[b_yaT], writes=[b_dbg])
            P.dma("sp", dbg[3, :, 0:8192], ybT[:].rearrange("p h t -> p (h t)"), reads=[b_ybT], writes=[b_dbg])
        P.emit()

    qbT_cm.__exit__(None, None, None)
    qaT_cm.__exit__(None, None, None)

    mT_cm = nc.sbuf_tensor("mT", [128, 16, T], BF16, side="right")
    mT = mT_cm.__enter__()
    b_mT = P.buf("mT")
    wo_cm = nc.sbuf_tensor("wo", [128, 2, 16, 512], BF16, side="right")
    wo = wo_cm.__enter__()
    b_wo = [P.buf("wo0"), P.buf("wo1")]

    def load_wo(n):
        P.dma("pool", wo[:, n % 2], w_out[:, n * 512:(n + 1) * 512].rearrange("(k p) c -> p k c", p=128), writes=[b_wo[n % 2]])

    sc = Scope(nc)
    wg = sc.t("wg", [128, 2, 2, 16, 256], BF16)
    wbr = sc.t("wbr", [128, 2, 2, 8, 256], BF16)
    sg = sc.t("sg", [128, 2, 2, 512], F32)
    m12 = sc.t("m12", [128, 2, 2, 512], F32)
    with sc:
        xT_sb = xT2
        b_xT = b_xT2
        b_wg = [P.buf("wg0"), P.buf("wg1")]
        b_sg = [P.buf("sg0"), P.buf("sg1")]
        b_m12 = [P.buf("m0"), P.buf("m1")]

        def load_g(g):
            s = g % 2
            c0 = g * 256
            P.dma("pool", wg[:, s, 0], w_in[:, 6144 + c0:6144 + c0 + 256].rearrange("(k p) c -> p k c", p=128), writes=[b_wg[s]])
            P.dma("pool", wg[:, s, 1], w_in[:, 8192 + c0:8192 + c0 + 256].rearrange("(k p) c -> p k c", p=128), writes=[b_wg[s]])
            P.dma("pool", wbr[:, s, 0], w_ba[:, c0:c0 + 256].rearrange("(k p) c -> p k c", p=128), writes=[b_wg[s]])
            P.dma("pool", wbr[:, s, 1], w_bb[:, c0:c0 + 256].rearrange("(k p) c -> p k c", p=128), writes=[b_wg[s]])

        load_g(0)
        pr = Ring([0, 1, 2, 3, 4, 5, 6, 7])
        it = 0
        for g in range(8):
            if g + 1 < 8:
                load_g(g + 1)
            if g == 6:
                load_wo(0)
                load_wo(1)
            s = g % 2
            for f in range(2):
                ft = g * 2 + f
                for th in range(2):
                    tc = slice(th * 512, (th + 1) * 512)
                    i2 = it % 2
                    it += 1
                    pss = [pr.next() for _ in range(4)]

                    def mmg(e, pss=pss, s=s, f=f, tc=tc):
                        for a in range(2):
                            for k in range(16):
                                e.matmul(pss[a][0][:], lhsT=wg[:, s, a, k, f * 128:(f + 1) * 128], rhs=xT_sb[:, k, tc], start=(k == 0), stop=(k == 15))
                        for a, yT in ((0, yaT), (1, ybT)):
                            for k in range(8):
                                i = e.matmul(pss[2 + a][0][:], lhsT=wbr[:, s, a, k, f * 128:(f + 1) * 128], rhs=yT[:, k, tc], start=(k == 0), stop=(k == 7))
                        return i
                    P.op("pe", mmg, reads=[b_wg[s], b_xT, b_yaT, b_ybT], writes=[p[1] for p in pss])

                    def sgf(e, pss=pss, i2=i2):
                        e.activation(out=sg[:, i2, 0, :], in_=pss[0][0][:], func=AF.Sigmoid)
                        return e.activation(out=sg[:, i2, 1, :], in_=pss[1][0][:], func=AF.Sigmoid)
                    P.op("act", sgf, reads=[pss[0][1], pss[1][1]], writes=[b_sg[i2]])

                    def mf(e, pss=pss, i2=i2):
                        e.tensor_tensor(out=m12[:, i2, 0, :], in0=sg[:, i2, 0, :], in1=pss[2][0][:], op=ALU.mult)
                        return e.tensor_tensor(out=m12[:, i2, 1, :], in0=sg[:, i2, 1, :], in1=pss[3][0][:], op=ALU.mult)
                    P.op("dve", mf, reads=[b_sg[i2], pss[2][1], pss[3][1]], writes=[b_m12[i2]])
                    P.op("pool", lambda e, i2=i2, ft=ft, tc=tc: e.tensor_tensor(out=mT[:, ft, tc], in0=m12[:, i2, 0, :], in1=m12[:, i2, 1, :], op=ALU.add),
                         reads=[b_m12[i2]], writes=[b_mT])
        if DEBUG:
            P.dma("sp", dbg[4, :, :], mT[:].rearrange("p h t -> p (h t)"), reads=[b_mT], writes=[b_dbg])
        P.emit()
    xT2_cm.__exit__(None, None, None)
    ybT_cm.__exit__(None, None, None)
    yaT_cm.__exit__(None, None, None)

    x1T_cm = nc.sbuf_tensor("x1T", [128, 16, T], BF16)
    x1T = x1T_cm.__enter__()
    b_x1T = P.buf("x1T")

    def layer_norm(src, bsrc, stats, mv, bst, lng, b_lng, dst_bf=None, bdst=None):
        def st(e):
            for c in range(4):
                i = e.bn_stats(out=stats[:, c, :], in_=src[:, c * 512:(c + 1) * 512])
            return i
        P.op("dve", st, reads=[bsrc], writes=[bst])
        P.op("dve", lambda e: e.bn_aggr(out=mv[:, 0:2], in_=stats.rearrange("p c s -> p (c s)")), reads=[bst], writes=[bst])
        P.op("dve", lambda e: e.tensor_scalar(out=mv[:, 2:3], in0=mv[:, 1:2], scalar1=1e-5, scalar2=None, op0=ALU.add), reads=[bst], writes=[bst])
        P.op("act", lambda e: e.activation(out=mv[:, 2:3], in_=mv[:, 2:3], func=AF.Ln), reads=[bst], writes=[bst])
        P.op("act", lambda e: e.activation(out=mv[:, 2:3], in_=mv[:, 2:3], func=AF.Exp, scale=-0.5), reads=[bst], writes=[bst])
        P.op("dve", lambda e: e.scalar_tensor_tensor(out=mv[:, 3:4], in0=mv[:, 0:1], scalar=-1.0, in1=mv[:, 2:3], op0=ALU.mult, op1=ALU.mult),
             reads=[bst], writes=[bst])
        P.op("act", lambda e: e.activation(out=src, in_=src, func=AF.Identity, bias=mv[:, 3:4], scale=mv[:, 2:3]), reads=[bst], writes=[bsrc])
        P.op("dve", lambda e: e.tensor_tensor(out=src, in0=src, in1=lng[:, 0, :], op=ALU.mult), reads=[b_lng], writes=[bsrc])
        P.op("dve", lambda e: e.tensor_tensor(out=src, in0=src, in1=lng[:, 1, :], op=ALU.add), reads=[b_lng], writes=[bsrc])
        if dst_bf is not None:
            P.op("pool", lambda e: e.tensor_copy(out=dst_bf, in_=src), reads=[bsrc], writes=[bdst])

    sc = Scope(nc)
    xr = sc.t("xr", [128, 8, D], F32)
    x1b = sc.t("x1b", [128, 2, D], BF16)
    lng = sc.t("lng1", [128, 2, D], F32)
    stats = sc.t("stats", [128, 2, 4, 6], F32)
    mv = sc.t("mv", [128, 2, 4], F32)
    with sc:
        b_xr = [P.buf(f"xr{t}") for t in range(8)]
        b_x1b = [P.buf("x1b0"), P.buf("x1b1")]
        b_st = [P.buf("st0"), P.buf("st1")]
        b_lng = P.buf("lng1")
        b_x1d = P.buf("x1d")
        P.dma("sp", lng[:], lnp[0:2, :].partition_broadcast(128), writes=[b_lng])
        for t in range(8):
            P.dma("sp", xr[:, t, :], xtok[t * 128:(t + 1) * 128, :], writes=[b_xr[t]])
        pr = Ring([0, 1, 2, 3])
        tring = Ring([4, 5, 6, 7])
        ln_pending = []

        def ln_tile(t):
            i2 = t % 2
            layer_norm(xr[:, t, :], b_xr[t], stats[:, i2], mv[:, i2], b_st[i2], lng, b_lng, x1b[:, i2, :], b_x1b[i2])
            P.dma("sp", x1d_ap[t * 128:(t + 1) * 128, :], xr[:, t, :], reads=[b_xr[t]], writes=[b_x1d])

            def trs(t=t, i2=i2):
                for c4 in range(4):
                    tp, btp = tring.next()
                    tpb = tp[:].bitcast(BF16)

                    def tf3(e, tpb=tpb, c4=c4):
                        for c in range(4):
                            k = c4 * 4 + c
                            i = e.transpose(out=tpb[:, c * 128:(c + 1) * 128], in_=x1b[:, i2, k * 128:(k + 1) * 128], identity=ident[:])
                        return i
                    P.op("pe", tf3, reads=[b_x1b[i2], b_const], writes=[btp])
                    P.op("act", lambda e, tpb=tpb, c4=c4: e.copy(out=x1T[:, c4 * 4:(c4 + 1) * 4, t * 128:(t + 1) * 128],
                                                               in_=tpb[:, 0:512].rearrange("p (c q) -> p c q", q=128)),
                         reads=[btp], writes=[b_x1T])
            ln_pending.append(trs)

        for n in range(4):
            s = n % 2
            for t in range(8):
                ps, bps = pr.next()

                def mmo(e, ps=ps, s=s, t=t):
                    for k in range(16):
                        i = e.matmul(ps[:], lhsT=mT[:, k, t * 128:(t + 1) * 128], rhs=wo[:, s, k, :], start=(k == 0), stop=(k == 15))
                    return i
                P.op("pe", mmo, reads=[b_mT, b_wo[s]], writes=[bps])
                P.op("dve", lambda e, ps=ps, t=t, n=n: e.scalar_tensor_tensor(out=xr[:, t, n * 512:(n + 1) * 512], in0=xr[:, t, n * 512:(n + 1) * 512],
                                                                          scalar=ALPHA, in1=ps[:], op0=ALU.mult, op1=ALU.add),
                     reads=[bps], writes=[b_xr[t]])
                if n == 3:
                    while len(ln_pending) > 1:
                        ln_pending.pop(0)()
                    ln_tile(t)
            if n + 2 < 4:
                load_wo(n + 2)
        while ln_pending:
            ln_pending.pop(0)()
        if DEBUG:
            P.dma("sp", dbg[5, :, :], x1T[:].rearrange("p h t -> p (h t)"), reads=[b_x1T], writes=[b_dbg])
        P.emit()
    wo_cm.__exit__(None, None, None)
    mT_cm.__exit__(None, None, None)

    hT_cm = nc.sbuf_tensor("hT", [128, 44, T], BF16, side="right")
    hT = hT_cm.__enter__()
    w2_cm = nc.sbuf_tensor("w2", [128, 2, 22, 256], BF16, side="right")
    w2 = w2_cm.__enter__()
    b_hT = P.buf("hT")
    b_w2 = [P.buf("w2a"), P.buf("w2b")]

    def load_w2(i):
        n, kh = i // 2, i % 2
        P.dma("pool", w2[:, i % 2], w_f2[kh * 2816:(kh + 1) * 2816, n * 256:(n + 1) * 256].rearrange("(k p) c -> p k c", p=128),
              writes=[b_w2[i % 2]])

    sc = Scope(nc)
    w1 = sc.t("w1", [128, 2, 2, 16, 256], BF16)
    sgl = sc.t("sgl", [128, 2, 512], F32)
    with sc:
        b_w1 = [P.buf("w1a"), P.buf("w1b")]
        b_sgl = [P.buf("sgl0"), P.buf("sgl1")]

        def load_w1(g):
            s = g % 2
            c0 = g * 256
            P.dma("pool", w1[:, s, 0], w_f1[:, c0:c0 + 256].rearrange("(k p) c -> p k c", p=128), writes=[b_w1[s]])
            P.dma("pool", w1[:, s, 1], w_f1[:, FF + c0:FF + c0 + 256].rearrange("(k p) c -> p k c", p=128), writes=[b_w1[s]])

        load_w1(0)
        load_w1(1)
        pr = Ring([0, 1, 2, 3, 4, 5, 6, 7])
        it = 0
        for g in range(22):
            s = g % 2
            for f in range(2):
                hid = g * 2 + f
                for th in range(2):
                    tc = slice(th * 512, (th + 1) * 512)
                    i2 = it % 2
                    it += 1
                    pg = pr.next()
                    pu = pr.next()

                    def mmf1(e, pg=pg, pu=pu, s=s, f=f, tc=tc):
                        for a, pp in ((0, pg), (1, pu)):
                            for k in range(16):
                                i = e.matmul(pp[0][:], lhsT=w1[:, s, a, k, f * 128:(f + 1) * 128], rhs=x1T[:, k, tc], start=(k == 0), stop=(k == 15))
                        return i
                    P.op("pe", mmf1, reads=[b_w1[s], b_x1T], writes=[pg[1], pu[1]])
                    P.op("act", lambda e, pg=pg, i2=i2: e.activation(out=sgl[:, i2, :], in_=pg[0][:], func=AF.Silu), reads=[pg[1]], writes=[b_sgl[i2]])
                    P.op("dve", lambda e, pu=pu, i2=i2, hid=hid, tc=tc: e.tensor_tensor(out=hT[:, hid, tc], in0=sgl[:, i2, :], in1=pu[0][:], op=ALU.mult),
                         reads=[b_sgl[i2], pu[1]], writes=[b_hT])
            if g + 2 < 22:
                load_w1(g + 2)
            if g == 19:
                load_w2(0)
            if g == 20:
                load_w2(1)
        P.emit()
    x1T_cm.__exit__(None, None, None)

    sc = Scope(nc)
    rr = sc.t("rr", [128, 8, D], F32)
    lng = sc.t("lng2", [128, 2, D], F32)
    stats = sc.t("stats2", [128, 2, 4, 6], F32)
    mv = sc.t("mv2", [128, 2, 4], F32)
    with sc:
        b_rr = [P.buf(f"rr{t}") for t in range(8)]
        b_st = [P.buf("st20"), P.buf("st21")]
        b_lng = P.buf("lng2")
        b_y = P.buf("y")
        for t in range(8):
            P.dma("sp", rr[:, t, :], x1d_ap[t * 128:(t + 1) * 128, :], writes=[b_rr[t]])
        P.dma("sp", lng[:], lnp[2:4, :].partition_broadcast(128), writes=[b_lng])
        for n in range(8):
            for kh in range(2):
                i = n * 2 + kh
                s = i % 2
                for t in ((0, 2, 4, 6, 1, 3, 5, 7) if kh == 1 else range(8)):
                    b = (n % 2) * 4 + t // 2
                    c0 = (t % 2) * 256
                    ps = banks[b]

                    def mmf2(e, ps=ps, s=s, t=t, kh=kh, c0=c0):
                        for k in range(22):
                            i_ = e.matmul(ps[:, c0:c0 + 256], lhsT=hT[:, kh * 22 + k, t * 128:(t + 1) * 128], rhs=w2[:, s, k, :],
                                          start=(kh == 0 and k == 0 and t % 2 == 0), stop=True, skip_group_check=True)
                        return i_
                    P.op("pe", mmf2, reads=[b_hT, b_w2[s]], writes=[bankb[b]])
                    if kh == 1:
                        P.op("dve", lambda e, ps=ps, t=t, n=n, c0=c0: e.scalar_tensor_tensor(
                            out=rr[:, t, n * 256:(n + 1) * 256], in0=rr[:, t, n * 256:(n + 1) * 256], scalar=ALPHA, in1=ps[:, c0:c0 + 256],
                            op0=ALU.mult, op1=ALU.add), reads=[bankb[b]], writes=[b_rr[t]])
                        if n == 7:
                            i2 = t % 2
                            layer_norm(rr[:, t, :], b_rr[t], stats[:, i2], mv[:, i2], b_st[i2], lng, b_lng)
                            P.dma("sp", y[t * 128:(t + 1) * 128, :], rr[:, t, :], reads=[b_rr[t]], writes=[b_y])
                if i + 2 < 16:
                    load_w2(i + 2)
        if DEBUG:
            P.dma("sp", dbgx1, x1d_ap, writes=[b_dbg])
        P.emit()
    w2_cm.__exit__(None, None, None)
    hT_cm.__exit__(None, None, None)
    return nc


_NC_CACHE = {}


def _tables(j):
    pos = np.concatenate([np.arange(j * 512, (j + 1) * 512), np.arange((7 - j) * 512, (8 - j) * 512)]).astype(np.float32)

    def rope(d):
        half = d // 2
        inv = (np.float32(10000.0) ** (-np.arange(half, dtype=np.float32) * np.float32(2.0) / np.float32(d))).astype(np.float32)
        ang = pos[:, None] * inv[None, :]
        c, s = np.cos(ang).astype(np.float32), np.sin(ang).astype(np.float32)
        return np.concatenate([c, c, -s, s], axis=1).astype(np.float32)
    ropeA, ropeB = rope(128), rope(64)
    vbias = np.full((8, 20), -1e30, np.float32)
    v01 = np.zeros((8, 20), np.float32)
    own = np.zeros((8, 20), np.float32)
    for t in range(8):
        hq, r = t // 4, t % 4
        segkb = j if hq == 0 else 7 - j
        for i in range(16):
            rr, hf = i // 4, (i // 2) % 2
            kb = rr if hf == 0 else 7 - rr
            if kb < segkb:
                vbias[t, i] = 0.0
                v01[t, i] = 1.0
        for l in range(4):
            lh, ls = l // 2, l % 2
            if lh == hq and ls == 0 and r >= 2:
                vbias[t, 16 + l] = 0.0
                v01[t, 16 + l] = 1.0
            if lh == hq and ls == r // 2:
                own[t, 16 + l] = 1.0
    mtab = np.concatenate([vbias.reshape(-1), v01.reshape(-1), own.reshape(-1)])[None, :].repeat(128, 0).astype(np.float32)
    slotb = np.zeros((128, 8), np.float32)
    for s in range(3):
        slotb[:, s] = 0.0 if s < j else NEG
    for kb in (4, 5, 6):
        slotb[:, 3 + kb - 4] = 0.0 if kb < 7 - j else NEG
    return ropeA, ropeB, mtab, slotb


def kernel(x, w_in, lambda_qk, diff_subln_w, w_branch_a, w_branch_b, w_out,
           ln1_g, ln1_b, w_ffn_in, w_ffn_out, ln2_g, ln2_b):
    x = np.asarray(x, np.float32)
    if "nc" not in _NC_CACHE:
        _NC_CACHE["nc"] = build_nc()
    nc = _NC_CACHE["nc"]
    kk = np.arange(128)
    tri = np.where(kk[:, None] <= kk[None, :], 0.0, NEG).astype(np.float32)
    cst = np.concatenate([tri, np.eye(128, dtype=np.float32)], axis=1)
    esel = np.zeros((128, 2560), np.float32)
    for i in range(20):
        esel[i, i * 128:(i + 1) * 128] = 1.0
    lnp = np.stack([np.asarray(a, np.float32).reshape(-1) for a in (ln1_g, ln1_b, ln2_g, ln2_b)], 0)
    shared = {
        "w_in": np.ascontiguousarray(np.asarray(w_in, np.float32)[0]),
        "w_ba": np.ascontiguousarray(np.asarray(w_branch_a, np.float32)[0]),
        "w_bb": np.ascontiguousarray(np.asarray(w_branch_b, np.float32)[0]),
        "w_out": np.ascontiguousarray(np.asarray(w_out, np.float32)[0]),
        "w_f1": np.ascontiguousarray(np.asarray(w_ffn_in, np.float32)[0]),
        "w_f2": np.ascontiguousarray(np.asarray(w_ffn_out, np.float32)[0]),
        "lnp": np.ascontiguousarray(lnp),
        "lam": np.ascontiguousarray(np.asarray(lambda_qk, np.float32).reshape(1, 256)),
        "subln": np.ascontiguousarray(np.asarray(diff_subln_w, np.float32).reshape(1, 128)),
        "cst": cst, "esel": esel,
    }
    in_maps = []
    for r in range(NCORES):
        b, j = r // 4, r % 4
        xt = np.concatenate([x[b, j * 512:(j + 1) * 512], x[b, (7 - j) * 512:(8 - j) * 512]], axis=0)
        ropeA, ropeB, mtab, slotb = _tables(j)
        m = dict(shared)
        m.update({"xtok": np.ascontiguousarray(xt), "xT": np.ascontiguousarray(xt.T),
                  "ropeA": ropeA, "ropeB": ropeB, "mtab": mtab, "slotb": slotb})
        in_maps.append(m)
    res = run_bass_kernel_spmd(nc, in_maps, core_ids=list(range(NCORES)))
    out = np.empty((2, 4096, D), np.float32)
    for r in range(NCORES):
        b, j = r // 4, r % 4
        yr = np.asarray(res.results[r]["y"], np.float32)
        out[b, j * 512:(j + 1) * 512] = yr[0:512]
        out[b, (7 - j) * 512:(8 - j) * 512] = yr[512:1024]
    if DEBUG:
        kernel.last = res
    return out
```

```python
import math
import numpy as np
import concourse.bass as bass
import concourse.mybir as mybir
from concourse.bass_utils import run_bass_kernel_spmd

F32 = mybir.dt.float32
BF16 = mybir.dt.bfloat16
AF = mybir.ActivationFunctionType
ALU = mybir.AluOpType
AX = mybir.AxisListType

NCORES = 8
D = 2048
T = 1024
NEG = -30000.0
ALPHA = 2.0 ** 0.25
LAM_INIT = 0.2
FF = 5632
DEBUG = False


class Buf:
    __slots__ = ("name", "writers", "readers", "sem", "count")

    def __init__(self, name):
        self.name = name
        self.writers = []
        self.readers = []
        self.sem = None
        self.count = 0


class Op:
    __slots__ = ("eng", "fn", "deps", "ms", "idx", "dma", "dsem", "dval", "inc")

    def __init__(self, eng, fn, dma=False):
        self.eng = eng
        self.fn = fn
        self.deps = []
        self.ms = False
        self.idx = None
        self.dma = dma
        self.dsem = None
        self.dval = 0
        self.inc = 16


ENGS = ("pe", "act", "dve", "pool", "sp")
ENGOBJ = {"pe": "tensor", "act": "scalar", "dve": "vector", "pool": "gpsimd", "sp": "sync"}


class Prog:
    def __init__(self, nc):
        self.nc = nc
        self.ops = {e: [] for e in ENGS}
        self.bufs = []
        self.esem = {e: nc.alloc_semaphore("e_" + e) for e in ENGS if e != "sp"}
        self.ecount = {e: 0 for e in ENGS}
        self.final = []

    def buf(self, name):
        b = Buf(name)
        self.bufs.append(b)
        return b

    def _track(self, op, reads, writes):
        deps = []
        for b in reads:
            deps.extend(b.writers)
        for b in writes:
            deps.extend(b.writers)
            deps.extend(b.readers)
        seen = set()
        for d in deps:
            if d is op or id(d) in seen:
                continue
            seen.add(id(d))
            if d.eng == "pe" and op.eng == "pe" and not d.dma:
                continue
            op.deps.append(d)
            if not d.dma:
                d.ms = True
        for b in reads:
            b.readers.append(op)
        for b in writes:
            b.writers = [op]
            b.readers = []

    def op(self, eng, fn, reads=(), writes=()):
        o = Op(eng, fn)
        self._track(o, reads, writes)
        self.ops[eng].append(o)
        return o

    def _own(self, o, owner):
        if owner.sem is None:
            owner.sem = self.nc.alloc_semaphore("d_" + owner.name)
        owner.count += 16
        o.dsem = owner.sem
        o.dval = owner.count

    def dma(self, q, out, in_, reads=(), writes=(), owner=None):
        o = Op(q, lambda e: e.dma_start(out=out, in_=in_), dma=True)
        self._own(o, owner if owner is not None else writes[0])
        self._track(o, reads, writes)
        self.ops[q].append(o)
        return o

    def cc(self, fn, reads=(), writes=()):
        o = Op("pool", fn, dma=True)
        self.ncc = getattr(self, "ncc", 0) + 1
        o.dsem = self.nc.alloc_semaphore(f"cc{self.ncc}")
        o.dval = 1
        o.inc = 1
        self._track(o, reads, writes)
        self.ops["pool"].append(o)
        return o

    def emit(self):
        nc = self.nc
        for e in ENGS:
            for o in self.ops[e]:
                if o.ms and not o.dma:
                    self.ecount[e] += 1
                    o.idx = self.ecount[e]
        dsems = {}
        for e in ENGS:
            for o in self.ops[e]:
                if o.dma:
                    k = id(o.dsem)
                    if k not in dsems or dsems[k][1] < o.dval:
                        dsems[k] = (o.dsem, o.dval)

        def run(e, eng):
            waited = {}
            for o in self.ops[e]:
                for d in o.deps:
                    if d.dma:
                        key, s, v = ("d", id(d.dsem)), d.dsem, d.dval
                    else:
                        key, s, v = ("e", d.eng), self.esem[d.eng], d.idx
                    if waited.get(key, 0) >= v:
                        continue
                    waited[key] = v
                    eng.wait_ge(s, v)
                inst = o.fn(eng)
                if o.dma:
                    inst.then_inc(o.dsem, o.inc)
                elif o.ms:
                    inst.then_inc(self.esem[e], 1)
            if e == "sp":
                for s, v in dsems.values():
                    eng.wait_ge(s, v)

        with nc.Block() as block:
            for e in ENGS:
                if not self.ops[e] and e != "sp":
                    continue
                getattr(block, ENGOBJ[e])(lambda eng, e=e: run(e, eng))
        self.ops = {e: [] for e in ENGS}
        for b in self.bufs:
            b.writers = []
            b.readers = []


class Scope:
    def __init__(self, nc):
        self.nc = nc
        self.cms = []

    def t(self, *a, **k):
        cm = self.nc.sbuf_tensor(*a, **k)
        self.cms.append(cm)
        return cm.__enter__()

    def __enter__(self):
        return self

    def __exit__(self, *exc):
        for cm in reversed(self.cms):
            cm.__exit__(None, None, None)
        return False


def build_nc():
    nc = bass.Bass("TRN2", target_bir_lowering=False)

    def din(name, shape, dt=F32):
        return nc.dram_tensor(name, list(shape), dt, kind="ExternalInput").ap()

    xT = din("xT", [D, T])
    xtok = din("xtok", [T, D])
    w_in = din("w_in", [D, 10240])
    w_ba = din("w_ba", [1024, D])
    w_bb = din("w_bb", [1024, D])
    w_out = din("w_out", [D, D])
    w_f1 = din("w_f1", [D, 2 * FF])
    w_f2 = din("w_f2", [FF, D])
    lnp = din("lnp", [4, D])
    lam_in = din("lam", [1, 256])
    subln = din("subln", [1, 128])
    ropeA = din("ropeA", [T, 256])
    ropeB = din("ropeB", [T, 128])
    mtab = din("mtab", [128, 3 * 160])
    slotb = din("slotb", [128, 8])
    cst = din("cst", [128, 256])
    esel_in = din("esel", [128, 2560])
    y = nc.dram_tensor("y", [T, D], F32, kind="ExternalOutput").ap()
    if DEBUG:
        dbg = nc.dram_tensor("dbg", [6, 128, 16 * 1024], BF16, kind="ExternalOutput").ap()
        dbgx1 = nc.dram_tensor("dbgx1", [T, D], F32, kind="ExternalOutput").ap()

    bK = [nc.dram_tensor(f"bK{c}", [512, 1024], BF16).ap() for c in range(4)]
    gK = [nc.dram_tensor(f"gK{c}", [4 * 512, 1024], BF16).ap() for c in range(4)]
    bV = [nc.dram_tensor(f"bV{c}", [1024, 512], BF16).ap() for c in range(4)]
    gV = [nc.dram_tensor(f"gV{c}", [4 * 1024, 512], BF16).ap() for c in range(4)]
    x1d = nc.dram_tensor("x1d", [T, D], F32)
    x1d_ap = x1d.ap()

    P = Prog(nc)
    sb = nc.alloc_sbuf_tensor

    banks = [nc.alloc_psum_tensor(f"bank{i}", [128, 512], F32) for i in range(8)]
    bankb = [P.buf(f"bank{i}") for i in range(8)]

    class Ring:
        def __init__(self, ids):
            self.ids = ids
            self.i = 0

        def next(self):
            k = self.ids[self.i % len(self.ids)]
            self.i += 1
            return banks[k], bankb[k]

    ident = sb("ident_sb", [128, 128], BF16)
    tri = sb("tri_sb", [128, 128], BF16)
    esel = sb("esel_sb", [128, 2560], BF16)
    b_const = P.buf("const")
    P.dma("pool", tri[:], cst[:, 0:128], writes=[b_const])
    P.dma("pool", ident[:], cst[:, 128:256], writes=[b_const])
    P.dma("pool", esel[:], esel_in, writes=[b_const])

    qaT_cm = nc.sbuf_tensor("qaT", [128, 8, T], BF16, side="right")
    qbT_cm = nc.sbuf_tensor("qbT", [128, 8, T], BF16, side="right")
    qaT = qaT_cm.__enter__()
    qbT = qbT_cm.__enter__()
    b_qaT = P.buf("qaT")
    b_qbT = P.buf("qbT")

    sc = Scope(nc)
    xT_sb = sc.t("xT_sb", [128, 16, T], BF16)
    wq = sc.t("wq", [128, 2, 16, 512], BF16)
    kaT = sc.t("kaT", [128, 8, T], BF16)
    kbT = sc.t("kbT", [128, 8, T], BF16)
    va_sb = sc.t("va", [128, 8, 1024], BF16)
    vb_sb = sc.t("vb", [128, 8, 1024], BF16)
    ropeA_sb = sc.t("ropeA_sb", [128, 8, 256], F32)
    ropeB_sb = sc.t("ropeB_sb", [128, 8, 128], F32)
    zf = sc.t("zf", [128, 3, 512], F32)
    tmp1 = sc.t("tmp1", [128, 2, 512], F32)
    tmp2 = sc.t("tmp2", [128, 2, 512], F32)
    zr = sc.t("zr", [128, 3, 512], BF16)
    with sc:
        b_xT = P.buf("xT")
        for q4 in range(4):
            P.dma("pool", xT_sb[:, q4 * 4:(q4 + 1) * 4, :],
                  xT[q4 * 512:(q4 + 1) * 512, :].rearrange("(k p) t -> p k t", p=128), writes=[b_xT])
        b_rope = P.buf("rope")
        P.dma("sp", ropeA_sb[:], ropeA.rearrange("(t p) c -> p t c", p=128), writes=[b_rope])
        P.dma("sp", ropeB_sb[:], ropeB.rearrange("(t p) c -> p t c", p=128), writes=[b_rope])
        b_w = [P.buf("wq0"), P.buf("wq1")]
        b_zf = [P.buf(f"zf{i}") for i in range(3)]
        b_t1 = [P.buf(f"t1{i}") for i in range(2)]
        b_t2 = [P.buf(f"t2{i}") for i in range(2)]
        b_zr = [P.buf(f"zr{i}") for i in range(3)]
        b_kT = [P.buf(f"kT{c}") for c in range(4)]
        b_vS = [P.buf(f"vS{c}") for c in range(4)]
        b_bK = [P.buf(f"bK{c}") for c in range(4)]
        b_bV = [P.buf(f"bV{c}") for c in range(4)]
        b_gK = [P.buf(f"gK{c}") for c in range(4)]
        b_gV = [P.buf(f"gV{c}") for c in range(4)]
        RG = [[0, 1, 2, 3], [4, 5, 6, 7]]
        zring = Ring([0, 1, 2, 3])
        tring = Ring([4, 5])
        kinds = ["qa", "qa", "ka", "ka", "va", "va", "qb", "qb", "kb", "kb", "vb", "vb"]
        dests = {"qa": (qaT, None), "ka": (kaT, 0), "qb": (qbT, None), "kb": (kbT, 2)}
        pending = []
        cnt = 0
        order = [2, 3, 8, 9, 4, 5, 10, 11, 0, 1, 6, 7]

        def flush(keep):
            while len(pending) > keep:
                pending.pop(0)()

        def load_w(idx):
            s = idx % 2
            n = order[idx]
            P.dma("pool", wq[:, s], w_in[:, n * 512:(n + 1) * 512].rearrange("(k p) c -> p k c", p=128),
                  writes=[b_w[s]])

        order = [2, 3, 8, 9, 4, 5, 10, 11, 0, 1, 6, 7]
        load_w(0)
        for idx in range(12):
            n = order[idx]
            if idx + 1 < 12:
                load_w(idx + 1)
            s = idx % 2
            kind = kinds[n]
            for t in range(8):
                ps, bps = zring.next()

                def mmf(e, ps=ps, s=s, t=t):
                    for k in range(16):
                        i = e.matmul(ps[:], lhsT=xT_sb[:, k, t * 128:(t + 1) * 128], rhs=wq[:, s, k, :],
                                     start=(k == 0), stop=(k == 15))
                    return i
                P.op("pe", mmf, reads=[b_xT, b_w[s]], writes=[bps])
                if kind in ("va", "vb"):
                    dst = va_sb if kind == "va" else vb_sb
                    bd = b_vS[(0 if kind == "va" else 2) + n % 2]
                    c0 = (n % 2) * 512
                    P.op("act", lambda e, ps=ps, dst=dst, t=t, c0=c0: e.copy(out=dst[:, t, c0:c0 + 512], in_=ps[:]),
                         reads=[bps], writes=[bd])
                    flush(1)
                    continue
                i3 = cnt % 3
                i2 = cnt % 2
                cnt += 1
                P.op("act", lambda e, ps=ps, i3=i3: e.copy(out=zf[:, i3, :], in_=ps[:]), reads=[bps], writes=[b_zf[i3]])
                if kind in ("qa", "ka"):
                    nh, hd, tab, tw = 4, 128, ropeA_sb, 128
                else:
                    nh, hd, tab, tw = 8, 64, ropeB_sb, 64
                hf = hd // 2
                z3 = zf[:, i3, :].rearrange("p (h d) -> p h d", d=hd)
                a3 = tmp1[:, i2, :].rearrange("p (h d) -> p h d", d=hd)
                c3 = tmp2[:, i2, :].rearrange("p (h d) -> p h d", d=hd)
                cosb = tab[:, t, 0:tw].unsqueeze(1).broadcast_to([128, nh, hd])
                sin_lo = tab[:, t, tw:tw + hf].unsqueeze(1).broadcast_to([128, nh, hf])
                sin_hi = tab[:, t, tw + hf:tw + hd].unsqueeze(1).broadcast_to([128, nh, hf])
                P.op("dve", lambda e, a3=a3, z3=z3, cosb=cosb: e.tensor_tensor(out=a3, in0=z3, in1=cosb, op=ALU.mult),
                     reads=[b_zf[i3], b_rope], writes=[b_t1[i2]])

                def r2(e, c3=c3, z3=z3, sin_lo=sin_lo, sin_hi=sin_hi, hf=hf, hd=hd):
                    e.tensor_tensor(out=c3[:, :, 0:hf], in0=z3[:, :, hf:hd], in1=sin_lo, op=ALU.mult)
                    return e.tensor_tensor(out=c3[:, :, hf:hd], in0=z3[:, :, 0:hf], in1=sin_hi, op=ALU.mult)
                P.op("dve", r2, reads=[b_zf[i3], b_rope], writes=[b_t2[i2]])
                P.op("dve", lambda e, i2=i2, i3=i3: e.tensor_tensor(out=zr[:, i3, :], in0=tmp1[:, i2, :], in1=tmp2[:, i2, :], op=ALU.add),
                     reads=[b_t1[i2], b_t2[i2]], writes=[b_zr[i3]])
                dst, cb = dests[kind]
                if cb is None:
                    bd = b_qaT if kind == "qa" else b_qbT
                else:
                    bd = b_kT[cb + n % 2]
                h0 = (n % 2) * 4

                def trans(i3=i3, dst=dst, bd=bd, h0=h0, t=t):
                    pt, bpt = tring.next()
                    ptb = pt[:].bitcast(BF16)

                    def tf(e):
                        for c in range(4):
                            i = e.transpose(out=ptb[:, c * 128:(c + 1) * 128], in_=zr[:, i3, c * 128:(c + 1) * 128], identity=ident[:])
                        return i
                    P.op("pe", tf, reads=[b_zr[i3], b_const], writes=[bpt])
                    P.op("act", lambda e: e.copy(out=dst[:, h0:h0 + 4, t * 128:(t + 1) * 128],
                                                 in_=ptb[:, 0:512].rearrange("p (c q) -> p c q", q=128)),
                         reads=[bpt], writes=[bd])
                pending.append(trans)
                flush(1)
            if kind in ("ka", "kb"):
                flush(0)
                c = (0 if kind == "ka" else 2) + n % 2
                src = (kaT if kind == "ka" else kbT)[:, (n % 2) * 4:(n % 2) * 4 + 4, :]
                P.dma("sp", bK[c].rearrange("(h p) t -> p h t", p=128), src, reads=[b_kT[c]], writes=[b_bK[c]])
                if kind == "ka":
                    P.cc(lambda e, c=c: e.collective_compute("AllGather", ALU.bypass, replica_groups=RG, ins=[bK[c]], outs=[gK[c]], dma_qos="P3"),
                         reads=[b_bK[c]], writes=[b_gK[c]])
            if kind in ("va", "vb"):
                c = (0 if kind == "va" else 2) + n % 2
                src = (va_sb if kind == "va" else vb_sb)[:, :, (n % 2) * 512:(n % 2) * 512 + 512]
                P.dma("sp", bV[c].rearrange("(t p) c -> p t c", p=128), src, reads=[b_vS[c]], writes=[b_bV[c]])
                if kind == "va":
                    P.cc(lambda e, c=c: e.collective_compute("AllGather", ALU.bypass, replica_groups=RG, ins=[bV[c]], outs=[gV[c]], dma_qos="P3"),
                         reads=[b_bV[c]], writes=[b_gV[c]])
        flush(0)
        if DEBUG:
            b_dbg = P.buf("dbg")
            P.dma("sp", dbg[0, :, 0:8192], qaT[:].rearrange("p h t -> p (h t)"), reads=[b_qaT], writes=[b_dbg])
            P.dma("sp", dbg[1, :, 0:8192], qbT[:].rearrange("p h t -> p (h t)"), reads=[b_qbT], writes=[b_dbg])
        P.emit()

    yaT_cm = nc.sbuf_tensor("yaT", [128, 8, T], BF16)
    ybT_cm = nc.sbuf_tensor("ybT", [128, 8, T], BF16)
    yaT = yaT_cm.__enter__()
    ybT = ybT_cm.__enter__()
    b_yaT, b_ybT = P.buf("yaT"), P.buf("ybT")
    xT2_cm = nc.sbuf_tensor("xT_sb2", [128, 16, T], BF16)
    xT2 = xT2_cm.__enter__()
    b_xT2 = P.buf("xT2")
    sc = Scope(nc)
    Kp = sc.t("Kp", [128, 2, 5, 1024], BF16)
    K1 = sc.t("K1", [128, 2, 5, 1024], BF16)
    Vp = sc.t("Vp", [128, 2, 5, 8, 258], BF16)
    pt = sc.t("pt", [128, 6, 512], BF16)
    mtab_sb = sc.t("mtab_sb", [128, 3, 8, 20], F32)
    slotb_sb = sc.t("slotb_sb", [128, 8], F32)
    lq = sc.t("lq", [128, 256], F32)
    sm = sc.t("sm", [128, 64], F32)
    wrow = sc.t("wrow", [128, 128], F32)
    kmT = sc.t("kmT", [128, 2, 20], BF16)
    ksum = sc.t("ksum", [128, 2, 20], F32)
    gate = sc.t("gate", [128, 2, 8, 20], F32)
    gsel = sc.t("gsel", [128, 2, 8, 20], F32)
    m8 = sc.t("m8", [128, 2, 8, 8], F32)
    mb = sc.t("mb", [128, 2, 8, 20], BF16)
    mbT = sc.t("mbT", [128, 2, T], BF16)
    rec = sc.t("rec", [128, 2, 8], F32)
    ssq = sc.t("ssq", [128, 2, 4], F32)
    of_ = sc.t("of", [128, 2, 4, 128], F32)
    junk = sc.t("junk", [128, 128], F32)
    ytok = sc.t("ytok", [128, 2, 4, 128], BF16)
    with sc:
        b_mtab, b_slotb, b_lq, b_sm, b_wrow = P.buf("mtab"), P.buf("slotb"), P.buf("lq"), P.buf("sm"), P.buf("wrow")
        P.dma("sp", mtab_sb[:].rearrange("p a t i -> p (a t i)"), mtab, writes=[b_mtab])
        P.dma("sp", slotb_sb[:], slotb, writes=[b_slotb])
        P.dma("sp", lq[:], lam_in.partition_broadcast(128).rearrange("p o c -> p (o c)"), writes=[b_lq])
        P.dma("sp", wrow[:], subln.partition_broadcast(128).rearrange("p o c -> p (o c)"), writes=[b_wrow])
        P.op("dve", lambda e: e.scalar_tensor_tensor(out=junk[:, 0:64], in0=lq[:, 0:64], scalar=1.0, in1=lq[:, 64:128], op0=ALU.mult, op1=ALU.mult, accum_out=sm[:, 0:1]), reads=[b_lq], writes=[b_sm])
        P.op("dve", lambda e: e.scalar_tensor_tensor(out=junk[:, 64:128], in0=lq[:, 128:192], scalar=1.0, in1=lq[:, 192:256], op0=ALU.mult, op1=ALU.mult, accum_out=sm[:, 1:2]), reads=[b_lq, b_sm], writes=[b_sm])
        P.op("act", lambda e: e.activation(out=sm[:, 2:4], in_=sm[:, 0:2], func=AF.Exp), reads=[b_sm], writes=[b_sm])
        P.op("dve", lambda e: e.tensor_tensor(out=sm[:, 4:5], in0=sm[:, 3:4], in1=sm[:, 2:3], op=ALU.subtract), reads=[b_sm], writes=[b_sm])
        P.op("dve", lambda e: e.tensor_scalar(out=sm[:, 4:5], in0=sm[:, 4:5], scalar1=-LAM_INIT, scalar2=None, op0=ALU.add), reads=[b_sm], writes=[b_sm])
        P.op("dve", lambda e: e.tensor_scalar(out=wrow[:], in0=wrow[:], scalar1=1.0 - LAM_INIT, scalar2=None, op0=ALU.mult), reads=[b_wrow], writes=[b_wrow])
        b_K = [P.buf("K0s"), P.buf("K1s")]
        b_V = [P.buf("V0s"), P.buf("V1s")]
        late_cc = {1: ("K", 2), 2: ("V", 2), 4: ("K", 3), 5: ("V", 3)}

        def issue_cc(kv, c):
            if kv == "K":
                P.cc(lambda e: e.collective_compute("AllGather", ALU.bypass, replica_groups=RG, ins=[bK[c]], outs=[gK[c]], dma_qos="P3"),
                     reads=[b_bK[c]], writes=[b_gK[c]])
            else:
                P.cc(lambda e: e.collective_compute("AllGather", ALU.bypass, replica_groups=RG, ins=[bV[c]], outs=[gV[c]], dma_qos="P3"),
                     reads=[b_bV[c]], writes=[b_gV[c]])
        P.op("pool", lambda e: e.memset(K1[:], 0.0), writes=b_K)
        P.op("pool", lambda e: e.memset(Vp[:], 1.0), writes=b_V)
        b_pt = [P.buf(f"pt{i}") for i in range(6)]
        b_small = [P.buf("small0"), P.buf("small1")]
        b_rec = P.buf("rec")
        b_mbT = [P.buf("mbT0"), P.buf("mbT1")]
        P.op("dve", lambda e: e.memset(mbT[:], 0.0), writes=b_mbT)
        b_fin = [P.buf("fin0"), P.buf("fin1")]
        sring = Ring([0, 1, 2, 3])
        def load_K(hidx):
            mx, h, ks = hidx // 8, hidx % 8, hidx % 2
            c = mx * 2 + h // 4
            r0 = (h % 4) * 128
            gv = gK[c].rearrange("(r x) t -> r x t", r=4)
            if mx == 0:
                P.dma("sp", Kp[:, ks, 0:4, :], gv[:, r0:r0 + 128, :].rearrange("r p t -> p r t"), reads=[b_gK[c]], writes=[b_K[ks]])
                P.dma("sp", Kp[:, ks, 4, :], bK[c][r0:r0 + 128, :], writes=[b_K[ks]])
            else:
                if hidx in (8, 9):
                    P.op("pool", lambda e, ks=ks: e.memset(Kp[64:128, ks], 0.0), writes=[b_K[ks]])
                for (Kt, lo) in ((Kp, 0), (K1, 64)):
                    P.dma("sp", Kt[lo:lo + 64, ks, 0:4, :], gv[:, r0 + lo:r0 + lo + 64, :].rearrange("r p t -> p r t"), reads=[b_gK[c]], writes=[b_K[ks]])
                    P.dma("sp", Kt[lo:lo + 64, ks, 4, :], bK[c][r0 + lo:r0 + lo + 64, :], writes=[b_K[ks]])

        def load_V(pi):
            mx, hp, vs = pi // 4, pi % 4, pi % 2
            c = mx * 2 + hp // 2
            c0 = (hp % 2) * 256
            gv = gV[c].rearrange("(r x) d -> r x d", r=4)
            for rr in range(4):
                P.dma("sp", Vp[:, vs, rr, :, 1:257], gv[rr, :, c0:c0 + 256].rearrange("(c p) d -> p c d", p=128), reads=[b_gV[c]], writes=[b_V[vs]])
            P.dma("sp", Vp[:, vs, 4, :, 1:257], bV[c][:, c0:c0 + 256].rearrange("(c p) d -> p c d", p=128), writes=[b_V[vs]])

        def kcol(kb):
            return (kb, 0) if kb < 4 else (7 - kb, 1)

        load_K(0)
        load_V(0)
        ptc = [0]
        ocnt = [0]
        fin_pending = []

        def flush_fin():
            while fin_pending:
                fin_pending.pop(0)()
        gate_ps = {}

        def gate_stage(h, stg):
            g = h % 2
            slot = h % 2
            bs = b_small[g]
            if stg == 0:
                kv = Kp[:, slot, 0:4, :].rearrange("p r (b k) -> p r b k", k=256)
                P.op("dve", lambda e: e.tensor_reduce(out=ksum[:, g, 0:16].rearrange("p (r b) -> p r b", b=4), in_=kv, axis=AX.X, op=ALU.add),
                     reads=[b_K[slot]], writes=[bs])
                kv2 = Kp[:, slot, 4, :].rearrange("p (b k) -> p b k", k=256)
                P.op("dve", lambda e: e.tensor_reduce(out=ksum[:, g, 16:20], in_=kv2, axis=AX.X, op=ALU.add),
                     reads=[b_K[slot], bs], writes=[bs])
                P.op("dve", lambda e: e.tensor_scalar(out=kmT[:, g, :], in0=ksum[:, g, :], scalar1=1.0 / 256.0, scalar2=None, op0=ALU.mult),
                     reads=[bs], writes=[bs])
            elif stg == 1:
                gp, bgp = sring.next()

                def gf(e):
                    for t in range(8):
                        i = e.matmul(gp[:, t * 20:(t + 1) * 20], lhsT=qaT[:, h, t * 128:(t + 1) * 128], rhs=kmT[:, g, :], start=True, stop=True)
                    return i
                P.op("pe", gf, reads=[b_qaT, bs], writes=[bgp])
                P.op("dve", lambda e: e.tensor_tensor(out=gate[:, g].rearrange("p t i -> p (t i)"), in0=gp[:, 0:160],
                                                      in1=mtab_sb[:, 0].rearrange("p t i -> p (t i)"), op=ALU.add),
                     reads=[bgp, b_mtab], writes=[bs])

                def selop1(e):
                    for t in range(8):
                        i = e.max(out=m8[:, g, t, :], in_=gate[:, g, t, :])
                    return i

                def selop2(e):
                    for t in range(8):
                        i = e.tensor_scalar(out=gsel[:, g, t, :], in0=gate[:, g, t, :], scalar1=m8[:, g, t, 2:3], scalar2=None, op0=ALU.is_ge)
                    return i
                P.op("dve", selop1, reads=[bs], writes=[bs])
                P.op("dve", selop2, reads=[bs], writes=[bs])
                P.op("dve", lambda e: e.tensor_tensor(out=gsel[:, g], in0=gsel[:, g], in1=mtab_sb[:, 1], op=ALU.mult), reads=[bs, b_mtab], writes=[bs])
                P.op("dve", lambda e: e.tensor_tensor(out=gsel[:, g], in0=gsel[:, g], in1=mtab_sb[:, 2], op=ALU.add), reads=[bs, b_mtab], writes=[bs])
                P.op("dve", lambda e: e.tensor_scalar(out=mb[:, g], in0=gsel[:, g], scalar1=-NEG, scalar2=NEG, op0=ALU.mult, op1=ALU.add),
                     reads=[bs], writes=[bs])
            else:
                tp, btp = sring.next()
                tpb = tp[:].bitcast(BF16)

                def tf(e):
                    for t in range(8):
                        i = e.transpose(out=tpb[0:20, t * 128:(t + 1) * 128], in_=mb[:, g, t, :], identity=ident[:])
                    return i
                P.op("pe", tf, reads=[bs, b_const], writes=[btp])
                P.op("act", lambda e: e.copy(out=mbT[0:20, g, :], in_=tpb[0:20, 0:1024]), reads=[btp], writes=[b_mbT[g]])

        for pi in range(8):
            mx, hp = pi // 4, pi % 4
            vslot = pi % 2
            if pi + 1 < 8:
                load_V(pi + 1)
            for hh in range(2):
                h = hp * 2 + hh
                hidx = pi * 2 + hh
                slot = hidx % 2
                if hidx + 1 < 16:
                    load_K(hidx + 1)
                if hidx in late_cc:
                    issue_cc(*late_cc[hidx])
                if hidx == 12:
                    for q4 in range(4):
                        P.dma("pool", xT2[:, q4 * 4:(q4 + 1) * 4, :],
                              xT[q4 * 512:(q4 + 1) * 512, :].rearrange("(k p) t -> p k t", p=128), writes=[b_xT2])
                qT = qaT if mx == 0 else qbT
                bq = b_qaT if mx == 0 else b_qbT
                nsub = 1 if mx == 0 else 2
                scale = (128.0 ** -0.5) if mx == 0 else 0.125
                if mx == 0 and h == 0:
                    for stg in range(3):
                        gate_stage(0, stg)
                gpar = h % 2
                for seg in range(2):
                    q0 = seg * 512
                    offk = [0, 1, 2] if seg == 0 else [0, 1, 2, 3, 4, 5, 6]
                    steps = []
                    for kb in offk:
                        rr, hf = kcol(kb)
                        sbi = None
                        if seg == 0:
                            sbi = kb
                        elif kb >= 4:
                            sbi = 3 + kb - 4
                        for c in range(4):
                            steps.append((rr, hf * 512 + c * 128, hf * 4 + c, c, False, rr * 4 + hf * 2 + c // 2, sbi))
                    for c in range(4):
                        steps.append((4, seg * 512 + c * 128, seg * 4 + c, c, True, 16 + seg * 2 + c // 2, None))
                    if nsub == 1:
                        ob0 = 4 + 2 * (ocnt[0] % 2)
                        ocnt[0] += 1
                        obanks = [(banks[ob0], bankb[ob0]), (banks[ob0 + 1], bankb[ob0 + 1])]
                    else:
                        obanks = [(banks[4], bankb[4]), (banks[5], bankb[5]), (banks[6], bankb[6]), (banks[7], bankb[7])]
                    first_pv = [True]
                    prev = [None]

                    def do_pv(st, ptis, slot=vslot, hh=hh, nsub=nsub, obanks=obanks, first_pv=first_pv):
                        kblk, koff, vch, c, diag, ei, sbi = st
                        r0 = c if diag else 0
                        vs = Vp[:, slot, kblk, vch, 0:129] if hh == 0 else Vp[:, slot, kblk, vch, 129:258]
                        fp = first_pv[0]
                        first_pv[0] = False

                        def pvf(e):
                            for m in range(nsub):
                                for r in range(r0, 4):
                                    ob = obanks[m * 2 + r // 2][0]
                                    i = e.matmul(ob[:, (r % 2) * 256:(r % 2) * 256 + 129], lhsT=pt[:, ptis[m], r * 128:(r + 1) * 128],
                                                 rhs=vs, start=(fp and r % 2 == 0), stop=True, skip_group_check=True)
                            return i
                        P.op("pe", pvf, reads=[b_pt[i] for i in ptis] + [b_V[slot]], writes=[ob[1] for ob in obanks])

                    for si, st in enumerate(steps):
                        kblk, koff, vch, c, diag, ei, sbi = st
                        n0 = c * 128 if diag else 0
                        if si == 3:
                            flush_fin()
                        if mx == 0 and h + 1 < 8 and seg == 1 and si in (6, 14, 22):
                            gate_stage(h + 1, (6, 14, 22).index(si))
                        ptis = []
                        for m in range(nsub):
                            sp_, bsp = sring.next()
                            kt = (Kp if m == 0 else K1)[:, slot, kblk, koff:koff + 128]

                            def qk(e, sp_=sp_, kt=kt, n0=n0, diag=diag, ei=ei, qT=qT, h=h, q0=q0, mx=mx, c=c, gpar=gpar):
                                last_plain = (mx == 1 and not diag)
                                i = e.matmul(sp_[:, n0:512], lhsT=kt, rhs=qT[:, h, q0 + n0:q0 + 512], start=True, stop=last_plain)
                                if mx == 0:
                                    i = e.matmul(sp_[:, n0:512], lhsT=esel[:, ei * 128:(ei + 1) * 128], rhs=mbT[:, gpar, q0 + n0:q0 + 512],
                                                 start=False, stop=(not diag))
                                if diag:
                                    i = e.matmul(sp_[:, n0:n0 + 128], lhsT=ident[:], rhs=tri[:], start=False, stop=True)
                                return i
                            P.op("pe", qk, reads=[b_K[slot], bq, b_const] + ([b_mbT[gpar]] if mx == 0 else []), writes=[bsp])
                            pi_ = ptc[0] % 6
                            ptc[0] += 1
                            ptis.append(pi_)
                            if sbi is not None and mx == 1:
                                P.op("act", lambda e, sp_=sp_, pi_=pi_, n0=n0, sbi=sbi, scale=scale: e.activation(
                                    out=pt[:, pi_, n0:512], in_=sp_[:, n0:512], func=AF.Exp, bias=slotb_sb[:, sbi:sbi + 1], scale=scale),
                                    reads=[bsp, b_slotb], writes=[b_pt[pi_]])
                            else:
                                P.op("act", lambda e, sp_=sp_, pi_=pi_, n0=n0, scale=scale: e.activation(
                                    out=pt[:, pi_, n0:512], in_=sp_[:, n0:512], func=AF.Exp, scale=scale),
                                    reads=[bsp], writes=[b_pt[pi_]])
                        if prev[0] is not None:
                            do_pv(*prev[0])
                        prev[0] = (st, ptis)
                    do_pv(*prev[0])
                    fs = seg
                    bf_ = b_fin[fs]
                    if mx == 0:
                        sc = 0 if hh == 0 else 128
                        d0 = 1 if hh == 0 else 0
                        ob0, ob1 = obanks[0][0], obanks[1][0]
                        for r in range(4):
                            ob = (ob0, ob1)[r // 2]
                            o0 = (r % 2) * 256
                            P.op("dve", lambda e, ob=ob, o0=o0, r=r, sc=sc, fs=fs: e.reciprocal(out=rec[:, fs, r:r + 1], in_=ob[:, o0 + sc:o0 + sc + 1]),
                                 reads=[obanks[r // 2][1]], writes=[b_rec])
                            P.op("dve", lambda e, ob=ob, o0=o0, r=r, d0=d0, fs=fs: e.tensor_scalar(out=ytok[:, fs, r, :], in0=ob[:, o0 + d0:o0 + d0 + 128],
                                                                                           scalar1=rec[:, fs, r:r + 1], scalar2=None, op0=ALU.mult),
                                 reads=[obanks[r // 2][1], b_rec], writes=[bf_])
                    else:
                        sc = 0 if hh == 0 else 128
                        d0 = 1 if hh == 0 else 0
                        for r in range(4):
                            oA = obanks[0 + r // 2][0]
                            oB = obanks[2 + r // 2][0]
                            o0 = (r % 2) * 256
                            rd_ = [obanks[0 + r // 2][1], obanks[2 + r // 2][1]]

                            def f1(e, oA=oA, oB=oB, o0=o0, r=r, sc=sc, fs=fs):
                                e.reciprocal(out=rec[:, fs, r:r + 1], in_=oA[:, o0 + sc:o0 + sc + 1])
                                return e.reciprocal(out=rec[:, fs, 4 + r:5 + r], in_=oB[:, o0 + sc:o0 + sc + 1])
                            P.op("dve", f1, reads=rd_, writes=[b_rec])
                            P.op("dve", lambda e, r=r, fs=fs: e.tensor_scalar(out=rec[:, fs, 4 + r:5 + r], in0=rec[:, fs, 4 + r:5 + r], scalar1=sm[:, 4:5],
                                                                       scalar2=None, op0=ALU.mult), reads=[b_rec, b_sm], writes=[b_rec])
                            P.op("dve", lambda e, oA=oA, o0=o0, r=r, d0=d0, fs=fs: e.tensor_scalar(out=of_[:, fs, r, :], in0=oA[:, o0 + d0:o0 + d0 + 128],
                                                                                           scalar1=rec[:, fs, r:r + 1], scalar2=None, op0=ALU.mult),
                                 reads=rd_ + [b_rec], writes=[bf_])
                            P.op("dve", lambda e, oB=oB, o0=o0, r=r, d0=d0, fs=fs: e.scalar_tensor_tensor(out=of_[:, fs, r, :], in0=oB[:, o0 + d0:o0 + d0 + 128],
                                                                                                  scalar=rec[:, fs, 4 + r:5 + r], in1=of_[:, fs, r, :],
                                                                                                  op0=ALU.mult, op1=ALU.add),
                                 reads=rd_ + [b_rec, bf_], writes=[bf_])
                            P.op("dve", lambda e, r=r, fs=fs: e.scalar_tensor_tensor(out=junk[:], in0=of_[:, fs, r, :], scalar=1.0, in1=of_[:, fs, r, :], op0=ALU.mult, op1=ALU.mult, accum_out=ssq[:, fs, r:r + 1]),
                                 reads=[bf_], writes=[b_rec])
                        P.op("dve", lambda e, fs=fs: e.tensor_scalar(out=ssq[:, fs, :], in0=ssq[:, fs, :], scalar1=1.0 / 128.0, scalar2=1e-5, op0=ALU.mult, op1=ALU.add),
                             reads=[b_rec], writes=[b_rec])
                        P.op("act", lambda e, fs=fs: e.activation(out=ssq[:, fs, :], in_=ssq[:, fs, :], func=AF.Ln), reads=[b_rec], writes=[b_rec])
                        P.op("act", lambda e, fs=fs: e.activation(out=ssq[:, fs, :], in_=ssq[:, fs, :], func=AF.Exp, scale=-0.5), reads=[b_rec], writes=[b_rec])
                        for r in range(4):
                            P.op("dve", lambda e, r=r, fs=fs: e.scalar_tensor_tensor(out=ytok[:, fs, r, :], in0=of_[:, fs, r, :], scalar=ssq[:, fs, r:r + 1], in1=wrow[:],
                                                                              op0=ALU.mult, op1=ALU.mult),
                                 reads=[bf_, b_rec, b_wrow], writes=[bf_])
                    def fin_t(fs=fs, bf_=bf_, mx=mx, h=h, q0=q0):
                        tp, btp = sring.next()
                        tpb = tp[:].bitcast(BF16)

                        def tf2(e):
                            for r in range(4):
                                i = e.transpose(out=tpb[:, r * 128:(r + 1) * 128], in_=ytok[:, fs, r, :], identity=ident[:])
                            return i
                        P.op("pe", tf2, reads=[bf_, b_const], writes=[btp])
                        yT, byT = (yaT, b_yaT) if mx == 0 else (ybT, b_ybT)
                        P.op("act", lambda e: e.copy(out=yT[:, h, q0:q0 + 512], in_=tpb[:, 0:512]), reads=[btp], writes=[byT])
                    fin_pending.append(fin_t)
        flush_fin()
        if DEBUG:
            P.dma("sp", dbg[2, :, 0:8192], yaT[:].rearrange("p h t -> p (h t)"), reads=[b_yaT], writes=[b_dbg])
            P.dma("sp", dbg[3, :, 0:8192], ybT[:].rearrange("p h t -> p (h t)"), reads=[b_ybT], writes=[b_dbg])
        P.emit()

    qbT_cm.__exit__(None, None, None)
    qaT_cm.__exit__(None, None, None)

    mT_cm = nc.sbuf_tensor("mT", [128, 16, T], BF16, side="right")
    mT = mT_cm.__enter__()
    b_mT = P.buf("mT")
    wo_cm = nc.sbuf_tensor("wo", [128, 2, 16, 512], BF16, side="right")
    wo = wo_cm.__enter__()
    b_wo = [P.buf("wo0"), P.buf("wo1")]

    def load_wo(n):
        P.dma("pool", wo[:, n % 2], w_out[:, n * 512:(n + 1) * 512].rearrange("(k p) c -> p k c", p=128), writes=[b_wo[n % 2]])

    sc = Scope(nc)
    wg = sc.t("wg", [128, 2, 2, 16, 256], BF16)
    wbr = sc.t("wbr", [128, 2, 2, 8, 256], BF16)
    sg = sc.t("sg", [128, 2, 2, 512], F32)
    m12 = sc.t("m12", [128, 2, 2, 512], F32)
    with sc:
        xT_sb = xT2
        b_xT = b_xT2
        b_wg = [P.buf("wg0"), P.buf("wg1")]
        b_sg = [P.buf("sg0"), P.buf("sg1")]
        b_m12 = [P.buf("m0"), P.buf("m1")]

        def load_g(g):
            s = g % 2
            c0 = g * 256
            P.dma("pool", wg[:, s, 0], w_in[:, 6144 + c0:6144 + c0 + 256].rearrange("(k p) c -> p k c", p=128), writes=[b_wg[s]])
            P.dma("pool", wg[:, s, 1], w_in[:, 8192 + c0:8192 + c0 + 256].rearrange("(k p) c -> p k c", p=128), writes=[b_wg[s]])
            P.dma("pool", wbr[:, s, 0], w_ba[:, c0:c0 + 256].rearrange("(k p) c -> p k c", p=128), writes=[b_wg[s]])
            P.dma("pool", wbr[:, s, 1], w_bb[:, c0:c0 + 256].rearrange("(k p) c -> p k c", p=128), writes=[b_wg[s]])

        load_g(0)
        pr = Ring([0, 1, 2, 3, 4, 5, 6, 7])
        it = 0
        for g in range(8):
            if g + 1 < 8:
                load_g(g + 1)
            if g == 6:
                load_wo(0)
                load_wo(1)
            s = g % 2
            for f in range(2):
                ft = g * 2 + f
                for th in range(2):
                    tc = slice(th * 512, (th + 1) * 512)
                    i2 = it % 2
                    it += 1
                    pss = [pr.next() for _ in range(4)]

                    def mmg(e, pss=pss, s=s, f=f, tc=tc):
                        for a in range(2):
                            for k in range(16):
                                e.matmul(pss[a][0][:], lhsT=wg[:, s, a, k, f * 128:(f + 1) * 128], rhs=xT_sb[:, k, tc], start=(k == 0), stop=(k == 15))
                        for a, yT in ((0, yaT), (1, ybT)):
                            for k in range(8):
                                i = e.matmul(pss[2 + a][0][:], lhsT=wbr[:, s, a, k, f * 128:(f + 1) * 128], rhs=yT[:, k, tc], start=(k == 0), stop=(k == 7))
                        return i
                    P.op("pe", mmg, reads=[b_wg[s], b_xT, b_yaT, b_ybT], writes=[p[1] for p in pss])

                    def sgf(e, pss=pss, i2=i2):
                        e.activation(out=sg[:, i2, 0, :], in_=pss[0][0][:], func=AF.Sigmoid)
                        return e.activation(out=sg[:, i2, 1, :], in_=pss[1][0][:], func=AF.Sigmoid)
                    P.op("act", sgf, reads=[pss[0][1], pss[1][1]], writes=[b_sg[i2]])

                    def mf(e, pss=pss, i2=i2):
                        e.tensor_tensor(out=m12[:, i2, 0, :], in0=sg[:, i2, 0, :], in1=pss[2][0][:], op=ALU.mult)
                        return e.tensor_tensor(out=m12[:, i2, 1, :], in0=sg[:, i2, 1, :], in1=pss[3][0][:], op=ALU.mult)
                    P.op("dve", mf, reads=[b_sg[i2], pss[2][1], pss[3][1]], writes=[b_m12[i2]])
                    P.op("pool", lambda e, i2=i2, ft=ft, tc=tc: e.tensor_tensor(out=mT[:, ft, tc], in0=m12[:, i2, 0, :], in1=m12[:, i2, 1, :], op=ALU.add),
                         reads=[b_m12[i2]], writes=[b_mT])
        if DEBUG:
            P.dma("sp", dbg[4, :, :], mT[:].rearrange("p h t -> p (h t)"), reads=[b_mT], writes=[b_dbg])
        P.emit()
    xT2_cm.__exit__(None, None, None)
    ybT_cm.__exit__(None, None, None)
    yaT_cm.__exit__(None, None, None)

    x1T_cm = nc.sbuf_tensor("x1T", [128, 16, T], BF16)
    x1T = x1T_cm.__enter__()
    b_x1T = P.buf("x1T")

    def layer_norm(src, bsrc, stats, mv, bst, lng, b_lng, dst_bf=None, bdst=None):
        def st(e):
            for c in range(4):
                i = e.bn_stats(out=stats[:, c, :], in_=src[:, c * 512:(c + 1) * 512])
            return i
        P.op("dve", st, reads=[bsrc], writes=[bst])
        P.op("dve", lambda e: e.bn_aggr(out=mv[:, 0:2], in_=stats.rearrange("p c s -> p (c s)")), reads=[bst], writes=[bst])
        P.op("dve", lambda e: e.tensor_scalar(out=mv[:, 2:3], in0=mv[:, 1:2], scalar1=1e-5, scalar2=None, op0=ALU.add), reads=[bst], writes=[bst])
        P.op("act", lambda e: e.activation(out=mv[:, 2:3], in_=mv[:, 2:3], func=AF.Ln), reads=[bst], writes=[bst])
        P.op("act", lambda e: e.activation(out=mv[:, 2:3], in_=mv[:, 2:3], func=AF.Exp, scale=-0.5), reads=[bst], writes=[bst])
        P.op("dve", lambda e: e.scalar_tensor_tensor(out=mv[:, 3:4], in0=mv[:, 0:1], scalar=-1.0, in1=mv[:, 2:3], op0=ALU.mult, op1=ALU.mult),
             reads=[bst], writes=[bst])
        P.op("act", lambda e: e.activation(out=src, in_=src, func=AF.Identity, bias=mv[:, 3:4], scale=mv[:, 2:3]), reads=[bst], writes=[bsrc])
        P.op("dve", lambda e: e.tensor_tensor(out=src, in0=src, in1=lng[:, 0, :], op=ALU.mult), reads=[b_lng], writes=[bsrc])
        P.op("dve", lambda e: e.tensor_tensor(out=src, in0=src, in1=lng[:, 1, :], op=ALU.add), reads=[b_lng], writes=[bsrc])
        if dst_bf is not None:
            P.op("pool", lambda e: e.tensor_copy(out=dst_bf, in_=src), reads=[bsrc], writes=[bdst])

    sc = Scope(nc)
    xr = sc.t("xr", [128, 8, D], F32)
    x1b = sc.t("x1b", [128, 2, D], BF16)
    lng = sc.t("lng1", [128, 2, D], F32)
    stats = sc.t("stats", [128, 2, 4, 6], F32)
    mv = sc.t("mv", [128, 2, 4], F32)
    with sc:
        b_xr = [P.buf(f"xr{t}") for t in range(8)]
        b_x1b = [P.buf("x1b0"), P.buf("x1b1")]
        b_st = [P.buf("st0"), P.buf("st1")]
        b_lng = P.buf("lng1")
        b_x1d = P.buf("x1d")
        P.dma("sp", lng[:], lnp[0:2, :].partition_broadcast(128), writes=[b_lng])
        for t in range(8):
            P.dma("sp", xr[:, t, :], xtok[t * 128:(t + 1) * 128, :], writes=[b_xr[t]])
        pr = Ring([0, 1, 2, 3])
        tring = Ring([4, 5, 6, 7])
        ln_pending = []

        def ln_tile(t):
            i2 = t % 2
            layer_norm(xr[:, t, :], b_xr[t], stats[:, i2], mv[:, i2], b_st[i2], lng, b_lng, x1b[:, i2, :], b_x1b[i2])
            P.dma("sp", x1d_ap[t * 128:(t + 1) * 128, :], xr[:, t, :], reads=[b_xr[t]], writes=[b_x1d])

            def trs(t=t, i2=i2):
                for c4 in range(4):
                    tp, btp = tring.next()
                    tpb = tp[:].bitcast(BF16)

                    def tf3(e, tpb=tpb, c4=c4):
                        for c in range(4):
                            k = c4 * 4 + c
                            i = e.transpose(out=tpb[:, c * 128:(c + 1) * 128], in_=x1b[:, i2, k * 128:(k + 1) * 128], identity=ident[:])
                        return i
                    P.op("pe", tf3, reads=[b_x1b[i2], b_const], writes=[btp])
                    P.op("act", lambda e, tpb=tpb, c4=c4: e.copy(out=x1T[:, c4 * 4:(c4 + 1) * 4, t * 128:(t + 1) * 128],
                                                               in_=tpb[:, 0:512].rearrange("p (c q) -> p c q", q=128)),
                         reads=[btp], writes=[b_x1T])
            ln_pending.append(trs)

        for n in range(4):
            s = n % 2
            for t in range(8):
                ps, bps = pr.next()

                def mmo(e, ps=ps, s=s, t=t):
                    for k in range(16):
                        i = e.matmul(ps[:], lhsT=mT[:, k, t * 128:(t + 1) * 128], rhs=wo[:, s, k, :], start=(k == 0), stop=(k == 15))
                    return i
                P.op("pe", mmo, reads=[b_mT, b_wo[s]], writes=[bps])
                P.op("dve", lambda e, ps=ps, t=t, n=n: e.scalar_tensor_tensor(out=xr[:, t, n * 512:(n + 1) * 512], in0=xr[:, t, n * 512:(n + 1) * 512],
                                                                          scalar=ALPHA, in1=ps[:], op0=ALU.mult, op1=ALU.add),
                     reads=[bps], writes=[b_xr[t]])
                if n == 3:
                    while len(ln_pending) > 1:
                        ln_pending.pop(0)()
                    ln_tile(t)
            if n + 2 < 4:
                load_wo(n + 2)
        while ln_pending:
            ln_pending.pop(0)()
        if DEBUG:
            P.dma("sp", dbg[5, :, :], x1T[:].rearrange("p h t -> p (h t)"), reads=[b_x1T], writes=[b_dbg])
        P.emit()
    wo_cm.__exit__(None, None, None)
    mT_cm.__exit__(None, None, None)

    hT_cm = nc.sbuf_tensor("hT", [128, 44, T], BF16, side="right")
    hT = hT_cm.__enter__()
    w2_cm = nc.sbuf_tensor("w2", [128, 2, 22, 256], BF16, side="right")
    w2 = w2_cm.__enter__()
    b_hT = P.buf("hT")
    b_w2 = [P.buf("w2a"), P.buf("w2b")]

    def load_w2(i):
        n, kh = i // 2, i % 2
        P.dma("pool", w2[:, i % 2], w_f2[kh * 2816:(kh + 1) * 2816, n * 256:(n + 1) * 256].rearrange("(k p) c -> p k c", p=128),
              writes=[b_w2[i % 2]])

    sc = Scope(nc)
    w1 = sc.t("w1", [128, 2, 2, 16, 256], BF16)
    sgl = sc.t("sgl", [128, 2, 512], F32)
    with sc:
        b_w1 = [P.buf("w1a"), P.buf("w1b")]
        b_sgl = [P.buf("sgl0"), P.buf("sgl1")]

        def load_w1(g):
            s = g % 2
            c0 = g * 256
            P.dma("pool", w1[:, s, 0], w_f1[:, c0:c0 + 256].rearrange("(k p) c -> p k c", p=128), writes=[b_w1[s]])
            P.dma("pool", w1[:, s, 1], w_f1[:, FF + c0:FF + c0 + 256].rearrange("(k p) c -> p k c", p=128), writes=[b_w1[s]])

        load_w1(0)
        load_w1(1)
        pr = Ring([0, 1, 2, 3, 4, 5, 6, 7])
        it = 0
        for g in range(22):
            s = g % 2
            for f in range(2):
                hid = g * 2 + f
                for th in range(2):
                    tc = slice(th * 512, (th + 1) * 512)
                    i2 = it % 2
                    it += 1
                    pg = pr.next()
                    pu = pr.next()

                    def mmf1(e, pg=pg, pu=pu, s=s, f=f, tc=tc):
                        for a, pp in ((0, pg), (1, pu)):
                            for k in range(16):
                                i = e.matmul(pp[0][:], lhsT=w1[:, s, a, k, f * 128:(f + 1) * 128], rhs=x1T[:, k, tc], start=(k == 0), stop=(k == 15))
                        return i
                    P.op("pe", mmf1, reads=[b_w1[s], b_x1T], writes=[pg[1], pu[1]])
                    P.op("act", lambda e, pg=pg, i2=i2: e.activation(out=sgl[:, i2, :], in_=pg[0][:], func=AF.Silu), reads=[pg[1]], writes=[b_sgl[i2]])
                    P.op("dve", lambda e, pu=pu, i2=i2, hid=hid, tc=tc: e.tensor_tensor(out=hT[:, hid, tc], in0=sgl[:, i2, :], in1=pu[0][:], op=ALU.mult),
                         reads=[b_sgl[i2], pu[1]], writes=[b_hT])
            if g + 2 < 22:
                load_w1(g + 2)
            if g == 19:
                load_w2(0)
            if g == 20:
                load_w2(1)
        P.emit()
    x1T_cm.__exit__(None, None, None)

    sc = Scope(nc)
    rr = sc.t("rr", [128, 8, D], F32)
    lng = sc.t("lng2", [128, 2, D], F32)
    stats = sc.t("stats2", [128, 2, 4, 6], F32)
    mv = sc.t("mv2", [128, 2, 4], F32)
    with sc:
        b_rr = [P.buf(f"rr{t}") for t in range(8)]
        b_st = [P.buf("st20"), P.buf("st21")]
        b_lng = P.buf("lng2")
        b_y = P.buf("y")
        for t in range(8):
            P.dma("sp", rr[:, t, :], x1d_ap[t * 128:(t + 1) * 128, :], writes=[b_rr[t]])
        P.dma("sp", lng[:], lnp[2:4, :].partition_broadcast(128), writes=[b_lng])
        for n in range(8):
            for kh in range(2):
                i = n * 2 + kh
                s = i % 2
                for t in ((0, 2, 4, 6, 1, 3, 5, 7) if kh == 1 else range(8)):
                    b = (n % 2) * 4 + t // 2
                    c0 = (t % 2) * 256
                    ps = banks[b]

                    def mmf2(e, ps=ps, s=s, t=t, kh=kh, c0=c0):
                        for k in range(22):
                            i_ = e.matmul(ps[:, c0:c0 + 256], lhsT=hT[:, kh * 22 + k, t * 128:(t + 1) * 128], rhs=w2[:, s, k, :],
                                          start=(kh == 0 and k == 0 and t % 2 == 0), stop=True, skip_group_check=True)
                        return i_
                    P.op("pe", mmf2, reads=[b_hT, b_w2[s]], writes=[bankb[b]])
                    if kh == 1:
                        P.op("dve", lambda e, ps=ps, t=t, n=n, c0=c0: e.scalar_tensor_tensor(
                            out=rr[:, t, n * 256:(n + 1) * 256], in0=rr[:, t, n * 256:(n + 1) * 256], scalar=ALPHA, in1=ps[:, c0:c0 + 256],
                            op0=ALU.mult, op1=ALU.add), reads=[bankb[b]], writes=[b_rr[t]])
                        if n == 7:
                            i2 = t % 2
                            layer_norm(rr[:, t, :], b_rr[t], stats[:, i2], mv[:, i2], b_st[i2], lng, b_lng)
                            P.dma("sp", y[t * 128:(t + 1) * 128, :], rr[:, t, :], reads=[b_rr[t]], writes=[b_y])
                if i + 2 < 16:
                    load_w2(i + 2)
        if DEBUG:
            P.dma("sp", dbgx1, x1d_ap, writes=[b_dbg])
        P.emit()
    w2_cm.__exit__(None, None, None)
    hT_cm.__exit__(None, None, None)
    return nc


_NC_CACHE = {}


def _tables(j):
    pos = np.concatenate([np.arange(j * 512, (j + 1) * 512), np.arange((7 - j) * 512, (8 - j) * 512)]).astype(np.float32)

    def rope(d):
        half = d // 2
        inv = (np.float32(10000.0) ** (-np.arange(half, dtype=np.float32) * np.float32(2.0) / np.float32(d))).astype(np.float32)
        ang = pos[:, None] * inv[None, :]
        c, s = np.cos(ang).astype(np.float32), np.sin(ang).astype(np.float32)
        return np.concatenate([c, c, -s, s], axis=1).astype(np.float32)
    ropeA, ropeB = rope(128), rope(64)
    vbias = np.full((8, 20), -1e30, np.float32)
    v01 = np.zeros((8, 20), np.float32)
    own = np.zeros((8, 20), np.float32)
    for t in range(8):
        hq, r = t // 4, t % 4
        segkb = j if hq == 0 else 7 - j
        for i in range(16):
            rr, hf = i // 4, (i // 2) % 2
            kb = rr if hf == 0 else 7 - rr
            if kb < segkb:
                vbias[t, i] = 0.0
                v01[t, i] = 1.0
        for l in range(4):
            lh, ls = l // 2, l % 2
            if lh == hq and ls == 0 and r >= 2:
                vbias[t, 16 + l] = 0.0
                v01[t, 16 + l] = 1.0
            if lh == hq and ls == r // 2:
                own[t, 16 + l] = 1.0
    mtab = np.concatenate([vbias.reshape(-1), v01.reshape(-1), own.reshape(-1)])[None, :].repeat(128, 0).astype(np.float32)
    slotb = np.zeros((128, 8), np.float32)
    for s in range(3):
        slotb[:, s] = 0.0 if s < j else NEG
    for kb in (4, 5, 6):
        slotb[:, 3 + kb - 4] = 0.0 if kb < 7 - j else NEG
    return ropeA, ropeB, mtab, slotb


def kernel(x, w_in, lambda_qk, diff_subln_w, w_branch_a, w_branch_b, w_out,
           ln1_g, ln1_b, w_ffn_in, w_ffn_out, ln2_g, ln2_b):
    x = np.asarray(x, np.float32)
    if "nc" not in _NC_CACHE:
        _NC_CACHE["nc"] = build_nc()
    nc = _NC_CACHE["nc"]
    kk = np.arange(128)
    tri = np.where(kk[:, None] <= kk[None, :], 0.0, NEG).astype(np.float32)
    cst = np.concatenate([tri, np.eye(128, dtype=np.float32)], axis=1)
    esel = np.zeros((128, 2560), np.float32)
    for i in range(20):
        esel[i, i * 128:(i + 1) * 128] = 1.0
    lnp = np.stack([np.asarray(a, np.float32).reshape(-1) for a in (ln1_g, ln1_b, ln2_g, ln2_b)], 0)
    shared = {
        "w_in": np.ascontiguousarray(np.asarray(w_in, np.float32)[0]),
        "w_ba": np.ascontiguousarray(np.asarray(w_branch_a, np.float32)[0]),
        "w_bb": np.ascontiguousarray(np.asarray(w_branch_b, np.float32)[0]),
        "w_out": np.ascontiguousarray(np.asarray(w_out, np.float32)[0]),
        "w_f1": np.ascontiguousarray(np.asarray(w_ffn_in, np.float32)[0]),
        "w_f2": np.ascontiguousarray(np.asarray(w_ffn_out, np.float32)[0]),
        "lnp": np.ascontiguousarray(lnp),
        "lam": np.ascontiguousarray(np.asarray(lambda_qk, np.float32).reshape(1, 256)),
        "subln": np.ascontiguousarray(np.asarray(diff_subln_w, np.float32).reshape(1, 128)),
        "cst": cst, "esel": esel,
    }
    in_maps = []
    for r in range(NCORES):
        b, j = r // 4, r % 4
        xt = np.concatenate([x[b, j * 512:(j + 1) * 512], x[b, (7 - j) * 512:(8 - j) * 512]], axis=0)
        ropeA, ropeB, mtab, slotb = _tables(j)
        m = dict(shared)
        m.update({"xtok": np.ascontiguousarray(xt), "xT": np.ascontiguousarray(xt.T),
                  "ropeA": ropeA, "ropeB": ropeB, "mtab": mtab, "slotb": slotb})
        in_maps.append(m)
    res = run_bass_kernel_spmd(nc, in_maps, core_ids=list(range(NCORES)))
    out = np.empty((2, 4096, D), np.float32)
    for r in range(NCORES):
        b, j = r // 4, r % 4
        yr = np.asarray(res.results[r]["y"], np.float32)
        out[b, j * 512:(j + 1) * 512] = yr[0:512]
        out[b, (7 - j) * 512:(8 - j) * 512] = yr[512:1024]
    if DEBUG:
        kernel.last = res
    return out
```

```python
import math
import numpy as np
import concourse.bass as bass
import concourse.mybir as mybir
from concourse.bass_utils import run_bass_kernel_spmd

F32 = mybir.dt.float32
BF16 = mybir.dt.bfloat16
AF = mybir.ActivationFunctionType
ALU = mybir.AluOpType
AX = mybir.AxisListType

NCORES = 8
D = 2048
T = 1024
NEG = -30000.0
ALPHA = 2.0 ** 0.25
LAM_INIT = 0.2
FF = 5632
DEBUG = False


class Buf:
    __slots__ = ("name", "writers", "readers", "sem", "count")

    def __init__(self, name):
        self.name = name
        self.writers = []
        self.readers = []
        self.sem = None
        self.count = 0


class Op:
    __slots__ = ("eng", "fn", "deps", "ms", "idx", "dma", "dsem", "dval", "inc")

    def __init__(self, eng, fn, dma=False):
        self.eng = eng
        self.fn = fn
        self.deps = []
        self.ms = False
        self.idx = None
        self.dma = dma
        self.dsem = None
        self.dval = 0
        self.inc = 16


ENGS = ("pe", "act", "dve", "pool", "sp")
ENGOBJ = {"pe": "tensor", "act": "scalar", "dve": "vector", "pool": "gpsimd", "sp": "sync"}


class Prog:
    def __init__(self, nc):
        self.nc = nc
        self.ops = {e: [] for e in ENGS}
        self.bufs = []
        self.esem = {e: nc.alloc_semaphore("e_" + e) for e in ENGS if e != "sp"}
        self.ecount = {e: 0 for e in ENGS}
        self.final = []

    def buf(self, name):
        b = Buf(name)
        self.bufs.append(b)
        return b

    def _track(self, op, reads, writes):
        deps = []
        for b in reads:
            deps.extend(b.writers)
        for b in writes:
            deps.extend(b.writers)
            deps.extend(b.readers)
        seen = set()
        for d in deps:
            if d is op or id(d) in seen:
                continue
            seen.add(id(d))
            if d.eng == "pe" and op.eng == "pe" and not d.dma:
                continue
            op.deps.append(d)
            if not d.dma:
                d.ms = True
        for b in reads:
            b.readers.append(op)
        for b in writes:
            b.writers = [op]
            b.readers = []

    def op(self, eng, fn, reads=(), writes=()):
        o = Op(eng, fn)
        self._track(o, reads, writes)
        self.ops[eng].append(o)
        return o

    def _own(self, o, owner):
        if owner.sem is None:
            owner.sem = self.nc.alloc_semaphore("d_" + owner.name)
        owner.count += 16
        o.dsem = owner.sem
        o.dval = owner.count

    def dma(self, q, out, in_, reads=(), writes=(), owner=None):
        o = Op(q, lambda e: e.dma_start(out=out, in_=in_), dma=True)
        self._own(o, owner if owner is not None else writes[0])
        self._track(o, reads, writes)
        self.ops[q].append(o)
        return o

    def cc(self, fn, reads=(), writes=()):
        o = Op("pool", fn, dma=True)
        self.ncc = getattr(self, "ncc", 0) + 1
        o.dsem = self.nc.alloc_semaphore(f"cc{self.ncc}")
        o.dval = 1
        o.inc = 1
        self._track(o, reads, writes)
        self.ops["pool"].append(o)
        return o

    def emit(self):
        nc = self.nc
        for e in ENGS:
            for o in self.ops[e]:
                if o.ms and not o.dma:
                    self.ecount[e] += 1
                    o.idx = self.ecount[e]
        dsems = {}
        for e in ENGS:
            for o in self.ops[e]:
                if o.dma:
                    k = id(o.dsem)
                    if k not in dsems or dsems[k][1] < o.dval:
                        dsems[k] = (o.dsem, o.dval)

        def run(e, eng):
            waited = {}
            for o in self.ops[e]:
                for d in o.deps:
                    if d.dma:
                        key, s, v = ("d", id(d.dsem)), d.dsem, d.dval
                    else:
                        key, s, v = ("e", d.eng), self.esem[d.eng], d.idx
                    if waited.get(key, 0) >= v:
                        continue
                    waited[key] = v
                    eng.wait_ge(s, v)
                inst = o.fn(eng)
                if o.dma:
                    inst.then_inc(o.dsem, o.inc)
                elif o.ms:
                    inst.then_inc(self.esem[e], 1)
            if e == "sp":
                for s, v in dsems.values():
                    eng.wait_ge(s, v)

        with nc.Block() as block:
            for e in ENGS:
                if not self.ops[e] and e != "sp":
                    continue
                getattr(block, ENGOBJ[e])(lambda eng, e=e: run(e, eng))
        self.ops = {e: [] for e in ENGS}
        for b in self.bufs:
            b.writers = []
            b.readers = []


class Scope:
    def __init__(self, nc):
        self.nc = nc
        self.cms = []

    def t(self, *a, **k):
        cm = self.nc.sbuf_tensor(*a, **k)
        self.cms.append(cm)
        return cm.__enter__()

    def __enter__(self):
        return self

    def __exit__(self, *exc):
        for cm in reversed(self.cms):
            cm.__exit__(None, None, None)
        return False


def build_nc():
    nc = bass.Bass("TRN2", target_bir_lowering=False)

    def din(name, shape, dt=F32):
        return nc.dram_tensor(name, list(shape), dt, kind="ExternalInput").ap()

    xT = din("xT", [D, T])
    xtok = din("xtok", [T, D])
    w_in = din("w_in", [D, 10240])
    w_ba = din("w_ba", [1024, D])
    w_bb = din("w_bb", [1024, D])
    w_out = din("w_out", [D, D])
    w_f1 = din("w_f1", [D, 2 * FF])
    w_f2 = din("w_f2", [FF, D])
    lnp = din("lnp", [4, D])
    lam_in = din("lam", [1, 256])
    subln = din("subln", [1, 128])
    ropeA = din("ropeA", [T, 256])
    ropeB = din("ropeB", [T, 128])
    mtab = din("mtab", [128, 3 * 160])
    slotb = din("slotb", [128, 8])
    cst = din("cst", [128, 256])
    esel_in = din("esel", [128, 2560])
    y = nc.dram_tensor("y", [T, D], F32, kind="ExternalOutput").ap()
    if DEBUG:
        dbg = nc.dram_tensor("dbg", [6, 128, 16 * 1024], BF16, kind="ExternalOutput").ap()
        dbgx1 = nc.dram_tensor("dbgx1", [T, D], F32, kind="ExternalOutput").ap()

    bK = [nc.dram_tensor(f"bK{c}", [512, 1024], BF16).ap() for c in range(4)]
    gK = [nc.dram_tensor(f"gK{c}", [4 * 512, 1024], BF16).ap() for c in range(4)]
    bV = [nc.dram_tensor(f"bV{c}", [1024, 512], BF16).ap() for c in range(4)]
    gV = [nc.dram_tensor(f"gV{c}", [4 * 1024, 512], BF16).ap() for c in range(4)]
    x1d = nc.dram_tensor("x1d", [T, D], F32)
    x1d_ap = x1d.ap()

    P = Prog(nc)
    sb = nc.alloc_sbuf_tensor

    banks = [nc.alloc_psum_tensor(f"bank{i}", [128, 512], F32) for i in range(8)]
    bankb = [P.buf(f"bank{i}") for i in range(8)]

    class Ring:
        def __init__(self, ids):
            self.ids = ids
            self.i = 0

        def next(self):
            k = self.ids[self.i % len(self.ids)]
            self.i += 1
            return banks[k], bankb[k]

    ident = sb("ident_sb", [128, 128], BF16)
    tri = sb("tri_sb", [128, 128], BF16)
    esel = sb("esel_sb", [128, 2560], BF16)
    b_const = P.buf("const")
    P.dma("pool", tri[:], cst[:, 0:128], writes=[b_const])
    P.dma("pool", ident[:], cst[:, 128:256], writes=[b_const])
    P.dma("pool", esel[:], esel_in, writes=[b_const])

    qaT_cm = nc.sbuf_tensor("qaT", [128, 8, T], BF16, side="right")
    qbT_cm = nc.sbuf_tensor("qbT", [128, 8, T], BF16, side="right")
    qaT = qaT_cm.__enter__()
    qbT = qbT_cm.__enter__()
    b_qaT = P.buf("qaT")
    b_qbT = P.buf("qbT")

    sc = Scope(nc)
    xT_sb = sc.t("xT_sb", [128, 16, T], BF16)
    wq = sc.t("wq", [128, 2, 16, 512], BF16)
    kaT = sc.t("kaT", [128, 8, T], BF16)
    kbT = sc.t("kbT", [128, 8, T], BF16)
    va_sb = sc.t("va", [128, 8, 1024], BF16)
    vb_sb = sc.t("vb", [128, 8, 1024], BF16)
    ropeA_sb = sc.t("ropeA_sb", [128, 8, 256], F32)
    ropeB_sb = sc.t("ropeB_sb", [128, 8, 128], F32)
    zf = sc.t("zf", [128, 3, 512], F32)
    tmp1 = sc.t("tmp1", [128, 2, 512], F32)
    tmp2 = sc.t("tmp2", [128, 2, 512], F32)
    zr = sc.t("zr", [128, 3, 512], BF16)
    with sc:
        b_xT = P.buf("xT")
        for q4 in range(4):
            P.dma("pool", xT_sb[:, q4 * 4:(q4 + 1) * 4, :],
                  xT[q4 * 512:(q4 + 1) * 512, :].rearrange("(k p) t -> p k t", p=128), writes=[b_xT])
        b_rope = P.buf("rope")
        P.dma("sp", ropeA_sb[:], ropeA.rearrange("(t p) c -> p t c", p=128), writes=[b_rope])
        P.dma("sp", ropeB_sb[:], ropeB.rearrange("(t p) c -> p t c", p=128), writes=[b_rope])
        b_w = [P.buf("wq0"), P.buf("wq1")]
        b_zf = [P.buf(f"zf{i}") for i in range(3)]
        b_t1 = [P.buf(f"t1{i}") for i in range(2)]
        b_t2 = [P.buf(f"t2{i}") for i in range(2)]
        b_zr = [P.buf(f"zr{i}") for i in range(3)]
        b_kT = [P.buf(f"kT{c}") for c in range(4)]
        b_vS = [P.buf(f"vS{c}") for c in range(4)]
        b_bK = [P.buf(f"bK{c}") for c in range(4)]
        b_bV = [P.buf(f"bV{c}") for c in range(4)]
        b_gK = [P.buf(f"gK{c}") for c in range(4)]
        b_gV = [P.buf(f"gV{c}") for c in range(4)]
        RG = [[0, 1, 2, 3], [4, 5, 6, 7]]
        zring = Ring([0, 1, 2, 3])
        tring = Ring([4, 5])
        kinds = ["qa", "qa", "ka", "ka", "va", "va", "qb", "qb", "kb", "kb", "vb", "vb"]
        dests = {"qa": (qaT, None), "ka": (kaT, 0), "qb": (qbT, None), "kb": (kbT, 2)}
        pending = []
        cnt = 0
        order = [2, 3, 8, 9, 4, 5, 10, 11, 0, 1, 6, 7]

        def flush(keep):
            while len(pending) > keep:
                pending.pop(0)()

        def load_w(idx):
            s = idx % 2
            n = order[idx]
            P.dma("pool", wq[:, s], w_in[:, n * 512:(n + 1) * 512].rearrange("(k p) c -> p k c", p=128),
                  writes=[b_w[s]])

        order = [2, 3, 8, 9, 4, 5, 10, 11, 0, 1, 6, 7]
        load_w(0)
        for idx in range(12):
            n = order[idx]
            if idx + 1 < 12:
                load_w(idx + 1)
            s = idx % 2
            kind = kinds[n]
            for t in range(8):
                ps, bps = zring.next()

                def mmf(e, ps=ps, s=s, t=t):
                    for k in range(16):
                        i = e.matmul(ps[:], lhsT=xT_sb[:, k, t * 128:(t + 1) * 128], rhs=wq[:, s, k, :],
                                     start=(k == 0), stop=(k == 15))
                    return i
                P.op("pe", mmf, reads=[b_xT, b_w[s]], writes=[bps])
                if kind in ("va", "vb"):
                    dst = va_sb if kind == "va" else vb_sb
                    bd = b_vS[(0 if kind == "va" else 2) + n % 2]
                    c0 = (n % 2) * 512
                    P.op("act", lambda e, ps=ps, dst=dst, t=t, c0=c0: e.copy(out=dst[:, t, c0:c0 + 512], in_=ps[:]),
                         reads=[bps], writes=[bd])
                    flush(1)
                    continue
                i3 = cnt % 3
                i2 = cnt % 2
                cnt += 1
                P.op("act", lambda e, ps=ps, i3=i3: e.copy(out=zf[:, i3, :], in_=ps[:]), reads=[bps], writes=[b_zf[i3]])
                if kind in ("qa", "ka"):
                    nh, hd, tab, tw = 4, 128, ropeA_sb, 128
                else:
                    nh, hd, tab, tw = 8, 64, ropeB_sb, 64
                hf = hd // 2
                z3 = zf[:, i3, :].rearrange("p (h d) -> p h d", d=hd)
                a3 = tmp1[:, i2, :].rearrange("p (h d) -> p h d", d=hd)
                c3 = tmp2[:, i2, :].rearrange("p (h d) -> p h d", d=hd)
                cosb = tab[:, t, 0:tw].unsqueeze(1).broadcast_to([128, nh, hd])
                sin_lo = tab[:, t, tw:tw + hf].unsqueeze(1).broadcast_to([128, nh, hf])
                sin_hi = tab[:, t, tw + hf:tw + hd].unsqueeze(1).broadcast_to([128, nh, hf])
                P.op("dve", lambda e, a3=a3, z3=z3, cosb=cosb: e.tensor_tensor(out=a3, in0=z3, in1=cosb, op=ALU.mult),
                     reads=[b_zf[i3], b_rope], writes=[b_t1[i2]])

                def r2(e, c3=c3, z3=z3, sin_lo=sin_lo, sin_hi=sin_hi, hf=hf, hd=hd):
                    e.tensor_tensor(out=c3[:, :, 0:hf], in0=z3[:, :, hf:hd], in1=sin_lo, op=ALU.mult)
                    return e.tensor_tensor(out=c3[:, :, hf:hd], in0=z3[:, :, 0:hf], in1=sin_hi, op=ALU.mult)
                P.op("dve", r2, reads=[b_zf[i3], b_rope], writes=[b_t2[i2]])
                P.op("dve", lambda e, i2=i2, i3=i3: e.tensor_tensor(out=zr[:, i3, :], in0=tmp1[:, i2, :], in1=tmp2[:, i2, :], op=ALU.add),
                     reads=[b_t1[i2], b_t2[i2]], writes=[b_zr[i3]])
                dst, cb = dests[kind]
                if cb is None:
                    bd = b_qaT if kind == "qa" else b_qbT
                else:
                    bd = b_kT[cb + n % 2]
                h0 = (n % 2) * 4

                def trans(i3=i3, dst=dst, bd=bd, h0=h0, t=t):
                    pt, bpt = tring.next()
                    ptb = pt[:].bitcast(BF16)

                    def tf(e):
                        for c in range(4):
                            i = e.transpose(out=ptb[:, c * 128:(c + 1) * 128], in_=zr[:, i3, c * 128:(c + 1) * 128], identity=ident[:])
                        return i
                    P.op("pe", tf, reads=[b_zr[i3], b_const], writes=[bpt])
                    P.op("act", lambda e: e.copy(out=dst[:, h0:h0 + 4, t * 128:(t + 1) * 128],
                                                 in_=ptb[:, 0:512].rearrange("p (c q) -> p c q", q=128)),
                         reads=[bpt], writes=[bd])
                pending.append(trans)
                flush(1)
            if kind in ("ka", "kb"):
                flush(0)
                c = (0 if kind == "ka" else 2) + n % 2
                src = (kaT if kind == "ka" else kbT)[:, (n % 2) * 4:(n % 2) * 4 + 4, :]
                P.dma("sp", bK[c].rearrange("(h p) t -> p h t", p=128), src, reads=[b_kT[c]], writes=[b_bK[c]])
                if True:
                    P.cc(lambda e, c=c: e.collective_compute("AllGather", ALU.bypass, replica_groups=RG, ins=[bK[c]], outs=[gK[c]], dma_qos="P3"),
                         reads=[b_bK[c]], writes=[b_gK[c]])
            if kind in ("va", "vb"):
                c = (0 if kind == "va" else 2) + n % 2
                src = (va_sb if kind == "va" else vb_sb)[:, :, (n % 2) * 512:(n % 2) * 512 + 512]
                P.dma("sp", bV[c].rearrange("(t p) c -> p t c", p=128), src, reads=[b_vS[c]], writes=[b_bV[c]])
                if True:
                    P.cc(lambda e, c=c: e.collective_compute("AllGather", ALU.bypass, replica_groups=RG, ins=[bV[c]], outs=[gV[c]], dma_qos="P3"),
                         reads=[b_bV[c]], writes=[b_gV[c]])
        flush(0)
        if DEBUG:
            b_dbg = P.buf("dbg")
            P.dma("sp", dbg[0, :, 0:8192], qaT[:].rearrange("p h t -> p (h t)"), reads=[b_qaT], writes=[b_dbg])
            P.dma("sp", dbg[1, :, 0:8192], qbT[:].rearrange("p h t -> p (h t)"), reads=[b_qbT], writes=[b_dbg])
        P.emit()

    yaT_cm = nc.sbuf_tensor("yaT", [128, 8, T], BF16)
    ybT_cm = nc.sbuf_tensor("ybT", [128, 8, T], BF16)
    yaT = yaT_cm.__enter__()
    ybT = ybT_cm.__enter__()
    b_yaT, b_ybT = P.buf("yaT"), P.buf("ybT")
    xT2_cm = nc.sbuf_tensor("xT_sb2", [128, 16, T], BF16)
    xT2 = xT2_cm.__enter__()
    b_xT2 = P.buf("xT2")
    sc = Scope(nc)
    Kp = sc.t("Kp", [128, 2, 5, 1024], BF16)
    K1 = sc.t("K1", [128, 2, 5, 1024], BF16)
    Vp = sc.t("Vp", [128, 2, 5, 8, 258], BF16)
    pt = sc.t("pt", [128, 6, 512], BF16)
    mtab_sb = sc.t("mtab_sb", [128, 3, 8, 20], F32)
    slotb_sb = sc.t("slotb_sb", [128, 8], F32)
    lq = sc.t("lq", [128, 256], F32)
    sm = sc.t("sm", [128, 64], F32)
    wrow = sc.t("wrow", [128, 128], F32)
    kmT = sc.t("kmT", [128, 2, 20], BF16)
    ksum = sc.t("ksum", [128, 2, 20], F32)
    gate = sc.t("gate", [128, 2, 8, 20], F32)
    gsel = sc.t("gsel", [128, 2, 8, 20], F32)
    m8 = sc.t("m8", [128, 2, 8, 8], F32)
    mb = sc.t("mb", [128, 2, 8, 20], BF16)
    mbT = sc.t("mbT", [128, 2, T], BF16)
    rec = sc.t("rec", [128, 2, 8], F32)
    ssq = sc.t("ssq", [128, 2, 4], F32)
    of_ = sc.t("of", [128, 2, 4, 128], F32)
    junk = sc.t("junk", [128, 128], F32)
    ytok = sc.t("ytok", [128, 2, 4, 128], BF16)
    with sc:
        b_mtab, b_slotb, b_lq, b_sm, b_wrow = P.buf("mtab"), P.buf("slotb"), P.buf("lq"), P.buf("sm"), P.buf("wrow")
        P.dma("sp", mtab_sb[:].rearrange("p a t i -> p (a t i)"), mtab, writes=[b_mtab])
        P.dma("sp", slotb_sb[:], slotb, writes=[b_slotb])
        P.dma("sp", lq[:], lam_in.partition_broadcast(128).rearrange("p o c -> p (o c)"), writes=[b_lq])
        P.dma("sp", wrow[:], subln.partition_broadcast(128).rearrange("p o c -> p (o c)"), writes=[b_wrow])
        P.op("dve", lambda e: e.scalar_tensor_tensor(out=junk[:, 0:64], in0=lq[:, 0:64], scalar=1.0, in1=lq[:, 64:128], op0=ALU.mult, op1=ALU.mult, accum_out=sm[:, 0:1]), reads=[b_lq], writes=[b_sm])
        P.op("dve", lambda e: e.scalar_tensor_tensor(out=junk[:, 64:128], in0=lq[:, 128:192], scalar=1.0, in1=lq[:, 192:256], op0=ALU.mult, op1=ALU.mult, accum_out=sm[:, 1:2]), reads=[b_lq, b_sm], writes=[b_sm])
        P.op("act", lambda e: e.activation(out=sm[:, 2:4], in_=sm[:, 0:2], func=AF.Exp), reads=[b_sm], writes=[b_sm])
        P.op("dve", lambda e: e.tensor_tensor(out=sm[:, 4:5], in0=sm[:, 3:4], in1=sm[:, 2:3], op=ALU.subtract), reads=[b_sm], writes=[b_sm])
        P.op("dve", lambda e: e.tensor_scalar(out=sm[:, 4:5], in0=sm[:, 4:5], scalar1=-LAM_INIT, scalar2=None, op0=ALU.add), reads=[b_sm], writes=[b_sm])
        P.op("dve", lambda e: e.tensor_scalar(out=wrow[:], in0=wrow[:], scalar1=1.0 - LAM_INIT, scalar2=None, op0=ALU.mult), reads=[b_wrow], writes=[b_wrow])
        b_K = [P.buf("K0s"), P.buf("K1s")]
        b_V = [P.buf("V0s"), P.buf("V1s")]
        late_cc = {1: ("K", 2), 2: ("V", 2), 4: ("K", 3), 5: ("V", 3)}

        def issue_cc(kv, c):
            if kv == "K":
                P.cc(lambda e: e.collective_compute("AllGather", ALU.bypass, replica_groups=RG, ins=[bK[c]], outs=[gK[c]], dma_qos="P3"),
                     reads=[b_bK[c]], writes=[b_gK[c]])
            else:
                P.cc(lambda e: e.collective_compute("AllGather", ALU.bypass, replica_groups=RG, ins=[bV[c]], outs=[gV[c]], dma_qos="P3"),
                     reads=[b_bV[c]], writes=[b_gV[c]])
        P.op("pool", lambda e: e.memset(K1[:], 0.0), writes=b_K)
        P.op("pool", lambda e: e.memset(Vp[:], 1.0), writes=b_V)
        b_pt = [P.buf(f"pt{i}") for i in range(6)]
        b_small = [P.buf("small0"), P.buf("small1")]
        b_rec = P.buf("rec")
        b_mbT = [P.buf("mbT0"), P.buf("mbT1")]
        P.op("dve", lambda e: e.memset(mbT[:], 0.0), writes=b_mbT)
        b_fin = [P.buf("fin0"), P.buf("fin1")]
        sring = Ring([0, 1, 2, 3])
        def load_K(hidx):
            mx, h, ks = hidx // 8, hidx % 8, hidx % 2
            c = mx * 2 + h // 4
            r0 = (h % 4) * 128
            gv = gK[c].rearrange("(r x) t -> r x t", r=4)
            if mx == 0:
                P.dma("sp", Kp[:, ks, 0:4, :], gv[:, r0:r0 + 128, :].rearrange("r p t -> p r t"), reads=[b_gK[c]], writes=[b_K[ks]])
                P.dma("sp", Kp[:, ks, 4, :], bK[c][r0:r0 + 128, :], writes=[b_K[ks]])
            else:
                if hidx in (8, 9):
                    P.op("pool", lambda e, ks=ks: e.memset(Kp[64:128, ks], 0.0), writes=[b_K[ks]])
                for (Kt, lo) in ((Kp, 0), (K1, 64)):
                    P.dma("sp", Kt[lo:lo + 64, ks, 0:4, :], gv[:, r0 + lo:r0 + lo + 64, :].rearrange("r p t -> p r t"), reads=[b_gK[c]], writes=[b_K[ks]])
                    P.dma("sp", Kt[lo:lo + 64, ks, 4, :], bK[c][r0 + lo:r0 + lo + 64, :], writes=[b_K[ks]])

        def load_V(pi):
            mx, hp, vs = pi // 4, pi % 4, pi % 2
            c = mx * 2 + hp // 2
            c0 = (hp % 2) * 256
            gv = gV[c].rearrange("(r x) d -> r x d", r=4)
            for rr in range(4):
                P.dma("sp", Vp[:, vs, rr, :, 1:257], gv[rr, :, c0:c0 + 256].rearrange("(c p) d -> p c d", p=128), reads=[b_gV[c]], writes=[b_V[vs]])
            P.dma("sp", Vp[:, vs, 4, :, 1:257], bV[c][:, c0:c0 + 256].rearrange("(c p) d -> p c d", p=128), writes=[b_V[vs]])

        def kcol(kb):
            return (kb, 0) if kb < 4 else (7 - kb, 1)

        load_K(0)
        load_V(0)
        ptc = [0]
        ocnt = [0]
        fin_pending = []

        def flush_fin():
            while fin_pending:
                fin_pending.pop(0)()
        gate_ps = {}

        def gate_stage(h, stg):
            g = h % 2
            slot = h % 2
            bs = b_small[g]
            if stg == 0:
                kv = Kp[:, slot, 0:4, :].rearrange("p r (b k) -> p r b k", k=256)
                P.op("dve", lambda e: e.tensor_reduce(out=ksum[:, g, 0:16].rearrange("p (r b) -> p r b", b=4), in_=kv, axis=AX.X, op=ALU.add),
                     reads=[b_K[slot]], writes=[bs])
                kv2 = Kp[:, slot, 4, :].rearrange("p (b k) -> p b k", k=256)
                P.op("dve", lambda e: e.tensor_reduce(out=ksum[:, g, 16:20], in_=kv2, axis=AX.X, op=ALU.add),
                     reads=[b_K[slot], bs], writes=[bs])
                P.op("dve", lambda e: e.tensor_scalar(out=kmT[:, g, :], in0=ksum[:, g, :], scalar1=1.0 / 256.0, scalar2=None, op0=ALU.mult),
                     reads=[bs], writes=[bs])
            elif stg == 1:
                gp, bgp = sring.next()

                def gf(e):
                    for t in range(8):
                        i = e.matmul(gp[:, t * 20:(t + 1) * 20], lhsT=qaT[:, h, t * 128:(t + 1) * 128], rhs=kmT[:, g, :], start=True, stop=True)
                    return i
                P.op("pe", gf, reads=[b_qaT, bs], writes=[bgp])
                P.op("dve", lambda e: e.tensor_tensor(out=gate[:, g].rearrange("p t i -> p (t i)"), in0=gp[:, 0:160],
                                                      in1=mtab_sb[:, 0].rearrange("p t i -> p (t i)"), op=ALU.add),
                     reads=[bgp, b_mtab], writes=[bs])

                def selop1(e):
                    for t in range(8):
                        i = e.max(out=m8[:, g, t, :], in_=gate[:, g, t, :])
                    return i

                def selop2(e):
                    for t in range(8):
                        i = e.tensor_scalar(out=gsel[:, g, t, :], in0=gate[:, g, t, :], scalar1=m8[:, g, t, 2:3], scalar2=None, op0=ALU.is_ge)
                    return i
                P.op("dve", selop1, reads=[bs], writes=[bs])
                P.op("dve", selop2, reads=[bs], writes=[bs])
                P.op("dve", lambda e: e.tensor_tensor(out=gsel[:, g], in0=gsel[:, g], in1=mtab_sb[:, 1], op=ALU.mult), reads=[bs, b_mtab], writes=[bs])
                P.op("dve", lambda e: e.tensor_tensor(out=gsel[:, g], in0=gsel[:, g], in1=mtab_sb[:, 2], op=ALU.add), reads=[bs, b_mtab], writes=[bs])
                P.op("dve", lambda e: e.tensor_scalar(out=mb[:, g], in0=gsel[:, g], scalar1=-NEG, scalar2=NEG, op0=ALU.mult, op1=ALU.add),
                     reads=[bs], writes=[bs])
            else:
                tp, btp = sring.next()
                tpb = tp[:].bitcast(BF16)

                def tf(e):
                    for t in range(8):
                        i = e.transpose(out=tpb[0:20, t * 128:(t + 1) * 128], in_=mb[:, g, t, :], identity=ident[:])
                    return i
                P.op("pe", tf, reads=[bs, b_const], writes=[btp])
                P.op("act", lambda e: e.copy(out=mbT[0:20, g, :], in_=tpb[0:20, 0:1024]), reads=[btp], writes=[b_mbT[g]])

        for pi in range(8):
            mx, hp = pi // 4, pi % 4
            vslot = pi % 2
            if pi + 1 < 8:
                load_V(pi + 1)
            for hh in range(2):
                h = hp * 2 + hh
                hidx = pi * 2 + hh
                slot = hidx % 2
                if hidx + 1 < 16:
                    load_K(hidx + 1)
                if hidx == 12:
                    for q4 in range(4):
                        P.dma("pool", xT2[:, q4 * 4:(q4 + 1) * 4, :],
                              xT[q4 * 512:(q4 + 1) * 512, :].rearrange("(k p) t -> p k t", p=128), writes=[b_xT2])
                qT = qaT if mx == 0 else qbT
                bq = b_qaT if mx == 0 else b_qbT
                nsub = 1 if mx == 0 else 2
                scale = (128.0 ** -0.5) if mx == 0 else 0.125
                if mx == 0 and h == 0:
                    for stg in range(3):
                        gate_stage(0, stg)
                gpar = h % 2
                for seg in range(2):
                    q0 = seg * 512
                    offk = [0, 1, 2] if seg == 0 else [0, 1, 2, 3, 4, 5, 6]
                    steps = []
                    for kb in offk:
                        rr, hf = kcol(kb)
                        sbi = None
                        if seg == 0:
                            sbi = kb
                        elif kb >= 4:
                            sbi = 3 + kb - 4
                        for c in range(4):
                            steps.append((rr, hf * 512 + c * 128, hf * 4 + c, c, False, rr * 4 + hf * 2 + c // 2, sbi))
                    for c in range(4):
                        steps.append((4, seg * 512 + c * 128, seg * 4 + c, c, True, 16 + seg * 2 + c // 2, None))
                    if nsub == 1:
                        ob0 = 4 + 2 * (ocnt[0] % 2)
                        ocnt[0] += 1
                        obanks = [(banks[ob0], bankb[ob0]), (banks[ob0 + 1], bankb[ob0 + 1])]
                    else:
                        obanks = [(banks[4], bankb[4]), (banks[5], bankb[5]), (banks[6], bankb[6]), (banks[7], bankb[7])]
                    first_pv = [True]
                    prev = [None]

                    def do_pv(st, ptis, slot=vslot, hh=hh, nsub=nsub, obanks=obanks, first_pv=first_pv):
                        kblk, koff, vch, c, diag, ei, sbi = st
                        r0 = c if diag else 0
                        vs = Vp[:, slot, kblk, vch, 0:129] if hh == 0 else Vp[:, slot, kblk, vch, 129:258]
                        fp = first_pv[0]
                        first_pv[0] = False

                        def pvf(e):
                            for m in range(nsub):
                                for r in range(r0, 4):
                                    ob = obanks[m * 2 + r // 2][0]
                                    i = e.matmul(ob[:, (r % 2) * 256:(r % 2) * 256 + 129], lhsT=pt[:, ptis[m], r * 128:(r + 1) * 128],
                                                 rhs=vs, start=(fp and r % 2 == 0), stop=True, skip_group_check=True)
                            return i
                        P.op("pe", pvf, reads=[b_pt[i] for i in ptis] + [b_V[slot]], writes=[ob[1] for ob in obanks])

                    for si, st in enumerate(steps):
                        kblk, koff, vch, c, diag, ei, sbi = st
                        n0 = c * 128 if diag else 0
                        if si == 3:
                            flush_fin()
                        if mx == 0 and h + 1 < 8 and seg == 1 and si in (6, 14, 22):
                            gate_stage(h + 1, (6, 14, 22).index(si))
                        ptis = []
                        for m in range(nsub):
                            sp_, bsp = sring.next()
                            kt = (Kp if m == 0 else K1)[:, slot, kblk, koff:koff + 128]

                            def qk(e, sp_=sp_, kt=kt, n0=n0, diag=diag, ei=ei, qT=qT, h=h, q0=q0, mx=mx, c=c, gpar=gpar):
                                last_plain = (mx == 1 and not diag)
                                i = e.matmul(sp_[:, n0:512], lhsT=kt, rhs=qT[:, h, q0 + n0:q0 + 512], start=True, stop=last_plain)
                                if mx == 0:
                                    i = e.matmul(sp_[:, n0:512], lhsT=esel[:, ei * 128:(ei + 1) * 128], rhs=mbT[:, gpar, q0 + n0:q0 + 512],
                                                 start=False, stop=(not diag))
                                if diag:
                                    i = e.matmul(sp_[:, n0:n0 + 128], lhsT=ident[:], rhs=tri[:], start=False, stop=True)
                                return i
                            P.op("pe", qk, reads=[b_K[slot], bq, b_const] + ([b_mbT[gpar]] if mx == 0 else []), writes=[bsp])
                            pi_ = ptc[0] % 6
                            ptc[0] += 1
                            ptis.append(pi_)
                            if sbi is not None and mx == 1:
                                P.op("act", lambda e, sp_=sp_, pi_=pi_, n0=n0, sbi=sbi, scale=scale: e.activation(
                                    out=pt[:, pi_, n0:512], in_=sp_[:, n0:512], func=AF.Exp, bias=slotb_sb[:, sbi:sbi + 1], scale=scale),
                                    reads=[bsp, b_slotb], writes=[b_pt[pi_]])
                            else:
                                P.op("act", lambda e, sp_=sp_, pi_=pi_, n0=n0, scale=scale: e.activation(
                                    out=pt[:, pi_, n0:512], in_=sp_[:, n0:512], func=AF.Exp, scale=scale),
                                    reads=[bsp], writes=[b_pt[pi_]])
                        if prev[0] is not None:
                            do_pv(*prev[0])
                        prev[0] = (st, ptis)
                    do_pv(*prev[0])
                    fs = seg
                    bf_ = b_fin[fs]
                    if mx == 0:
                        sc = 0 if hh == 0 else 128
                        d0 = 1 if hh == 0 else 0
                        ob0, ob1 = obanks[0][0], obanks[1][0]
                        for r in range(4):
                            ob = (ob0, ob1)[r // 2]
                            o0 = (r % 2) * 256
                            P.op("dve", lambda e, ob=ob, o0=o0, r=r, sc=sc, fs=fs: e.reciprocal(out=rec[:, fs, r:r + 1], in_=ob[:, o0 + sc:o0 + sc + 1]),
                                 reads=[obanks[r // 2][1]], writes=[b_rec])
                            P.op("dve", lambda e, ob=ob, o0=o0, r=r, d0=d0, fs=fs: e.tensor_scalar(out=ytok[:, fs, r, :], in0=ob[:, o0 + d0:o0 + d0 + 128],
                                                                                           scalar1=rec[:, fs, r:r + 1], scalar2=None, op0=ALU.mult),
                                 reads=[obanks[r // 2][1], b_rec], writes=[bf_])
                    else:
                        sc = 0 if hh == 0 else 128
                        d0 = 1 if hh == 0 else 0
                        for r in range(4):
                            oA = obanks[0 + r // 2][0]
                            oB = obanks[2 + r // 2][0]
                            o0 = (r % 2) * 256
                            rd_ = [obanks[0 + r // 2][1], obanks[2 + r // 2][1]]

                            def f1(e, oA=oA, oB=oB, o0=o0, r=r, sc=sc, fs=fs):
                                e.reciprocal(out=rec[:, fs, r:r + 1], in_=oA[:, o0 + sc:o0 + sc + 1])
                                return e.reciprocal(out=rec[:, fs, 4 + r:5 + r], in_=oB[:, o0 + sc:o0 + sc + 1])
                            P.op("dve", f1, reads=rd_, writes=[b_rec])
                            P.op("dve", lambda e, r=r, fs=fs: e.tensor_scalar(out=rec[:, fs, 4 + r:5 + r], in0=rec[:, fs, 4 + r:5 + r], scalar1=sm[:, 4:5],
                                                                       scalar2=None, op0=ALU.mult), reads=[b_rec, b_sm], writes=[b_rec])
                            P.op("dve", lambda e, oA=oA, o0=o0, r=r, d0=d0, fs=fs: e.tensor_scalar(out=of_[:, fs, r, :], in0=oA[:, o0 + d0:o0 + d0 + 128],
                                                                                           scalar1=rec[:, fs, r:r + 1], scalar2=None, op0=ALU.mult),
                                 reads=rd_ + [b_rec], writes=[bf_])
                            P.op("dve", lambda e, oB=oB, o0=o0, r=r, d0=d0, fs=fs: e.scalar_tensor_tensor(out=of_[:, fs, r, :], in0=oB[:, o0 + d0:o0 + d0 + 128],
                                                                                                  scalar=rec[:, fs, 4 + r:5 + r], in1=of_[:, fs, r, :],
                                                                                                  op0=ALU.mult, op1=ALU.add),
                                 reads=rd_ + [b_rec, bf_], writes=[bf_])
                            P.op("dve", lambda e, r=r, fs=fs: e.scalar_tensor_tensor(out=junk[:], in0=of_[:, fs, r, :], scalar=1.0, in1=of_[:, fs, r, :], op0=ALU.mult, op1=ALU.mult, accum_out=ssq[:, fs, r:r + 1]),
                                 reads=[bf_], writes=[b_rec])
                        P.op("dve", lambda e, fs=fs: e.tensor_scalar(out=ssq[:, fs, :], in0=ssq[:, fs, :], scalar1=1.0 / 128.0, scalar2=1e-5, op0=ALU.mult, op1=ALU.add),
                             reads=[b_rec], writes=[b_rec])
                        P.op("act", lambda e, fs=fs: e.activation(out=ssq[:, fs, :], in_=ssq[:, fs, :], func=AF.Ln), reads=[b_rec], writes=[b_rec])
                        P.op("act", lambda e, fs=fs: e.activation(out=ssq[:, fs, :], in_=ssq[:, fs, :], func=AF.Exp, scale=-0.5), reads=[b_rec], writes=[b_rec])
                        for r in range(4):
                            P.op("dve", lambda e, r=r, fs=fs: e.scalar_tensor_tensor(out=ytok[:, fs, r, :], in0=of_[:, fs, r, :], scalar=ssq[:, fs, r:r + 1], in1=wrow[:],
                                                                              op0=ALU.mult, op1=ALU.mult),
                                 reads=[bf_, b_rec, b_wrow], writes=[bf_])
                    def fin_t(fs=fs, bf_=bf_, mx=mx, h=h, q0=q0):
                        tp, btp = sring.next()
                        tpb = tp[:].bitcast(BF16)

                        def tf2(e):
                            for r in range(4):
                                i = e.transpose(out=tpb[:, r * 128:(r + 1) * 128], in_=ytok[:, fs, r, :], identity=ident[:])
                            return i
                        P.op("pe", tf2, reads=[bf_, b_const], writes=[btp])
                        yT, byT = (yaT, b_yaT) if mx == 0 else (ybT, b_ybT)
                        P.op("act", lambda e: e.copy(out=yT[:, h, q0:q0 + 512], in_=tpb[:, 0:512]), reads=[btp], writes=[byT])
                    fin_pending.append(fin_t)
        flush_fin()
        if DEBUG:
            P.dma("sp", dbg[2, :, 0:8192], yaT[:].rearrange("p h t -> p (h t)"), reads=[b_yaT], writes=[b_dbg])
            P.dma("sp", dbg[3, :, 0:8192], ybT[:].rearrange("p h t -> p (h t)"), reads=[b_ybT], writes=[b_dbg])
        P.emit()

    qbT_cm.__exit__(None, None, None)
    qaT_cm.__exit__(None, None, None)

    mT_cm = nc.sbuf_tensor("mT", [128, 16, T], BF16, side="right")
    mT = mT_cm.__enter__()
    b_mT = P.buf("mT")
    wo_cm = nc.sbuf_tensor("wo", [128, 2, 16, 512], BF16, side="right")
    wo = wo_cm.__enter__()
    b_wo = [P.buf("wo0"), P.buf("wo1")]

    def load_wo(n):
        P.dma("pool", wo[:, n % 2], w_out[:, n * 512:(n + 1) * 512].rearrange("(k p) c -> p k c", p=128), writes=[b_wo[n % 2]])

    sc = Scope(nc)
    wg = sc.t("wg", [128, 2, 2, 16, 256], BF16)
    wbr = sc.t("wbr", [128, 2, 2, 8, 256], BF16)
    sg = sc.t("sg", [128, 2, 2, 512], F32)
    m12 = sc.t("m12", [128, 2, 2, 512], F32)
    with sc:
        xT_sb = xT2
        b_xT = b_xT2
        b_wg = [P.buf("wg0"), P.buf("wg1")]
        b_sg = [P.buf("sg0"), P.buf("sg1")]
        b_m12 = [P.buf("m0"), P.buf("m1")]

        def load_g(g):
            s = g % 2
            c0 = g * 256
            P.dma("pool", wg[:, s, 0], w_in[:, 6144 + c0:6144 + c0 + 256].rearrange("(k p) c -> p k c", p=128), writes=[b_wg[s]])
            P.dma("pool", wg[:, s, 1], w_in[:, 8192 + c0:8192 + c0 + 256].rearrange("(k p) c -> p k c", p=128), writes=[b_wg[s]])
            P.dma("pool", wbr[:, s, 0], w_ba[:, c0:c0 + 256].rearrange("(k p) c -> p k c", p=128), writes=[b_wg[s]])
            P.dma("pool", wbr[:, s, 1], w_bb[:, c0:c0 + 256].rearrange("(k p) c -> p k c", p=128), writes=[b_wg[s]])

        load_g(0)
        pr = Ring([0, 1, 2, 3, 4, 5, 6, 7])
        it = 0
        for g in range(8):
            if g + 1 < 8:
                load_g(g + 1)
            if g == 6:
                load_wo(0)
                load_wo(1)
            s = g % 2
            for f in range(2):
                ft = g * 2 + f
                for th in range(2):
                    tc = slice(th * 512, (th + 1) * 512)
                    i2 = it % 2
                    it += 1
                    pss = [pr.next() for _ in range(4)]

                    def mmg(e, pss=pss, s=s, f=f, tc=tc):
                        for a in range(2):
                            for k in range(16):
                                e.matmul(pss[a][0][:], lhsT=wg[:, s, a, k, f * 128:(f + 1) * 128], rhs=xT_sb[:, k, tc], start=(k == 0), stop=(k == 15))
                        for a, yT in ((0, yaT), (1, ybT)):
                            for k in range(8):
                                i = e.matmul(pss[2 + a][0][:], lhsT=wbr[:, s, a, k, f * 128:(f + 1) * 128], rhs=yT[:, k, tc], start=(k == 0), stop=(k == 7))
                        return i
                    P.op("pe", mmg, reads=[b_wg[s], b_xT, b_yaT, b_ybT], writes=[p[1] for p in pss])

                    def sgf(e, pss=pss, i2=i2):
                        e.activation(out=sg[:, i2, 0, :], in_=pss[0][0][:], func=AF.Sigmoid)
                        return e.activation(out=sg[:, i2, 1, :], in_=pss[1][0][:], func=AF.Sigmoid)
                    P.op("act", sgf, reads=[pss[0][1], pss[1][1]], writes=[b_sg[i2]])

                    def mf(e, pss=pss, i2=i2):
                        e.tensor_tensor(out=m12[:, i2, 0, :], in0=sg[:, i2, 0, :], in1=pss[2][0][:], op=ALU.mult)
                        return e.tensor_tensor(out=m12[:, i2, 1, :], in0=sg[:, i2, 1, :], in1=pss[3][0][:], op=ALU.mult)
                    P.op("dve", mf, reads=[b_sg[i2], pss[2][1], pss[3][1]], writes=[b_m12[i2]])
                    P.op("pool", lambda e, i2=i2, ft=ft, tc=tc: e.tensor_tensor(out=mT[:, ft, tc], in0=m12[:, i2, 0, :], in1=m12[:, i2, 1, :], op=ALU.add),
                         reads=[b_m12[i2]], writes=[b_mT])
        if DEBUG:
            P.dma("sp", dbg[4, :, :], mT[:].rearrange("p h t -> p (h t)"), reads=[b_mT], writes=[b_dbg])
        P.emit()
    xT2_cm.__exit__(None, None, None)
    ybT_cm.__exit__(None, None, None)
    yaT_cm.__exit__(None, None, None)

    x1T_cm = nc.sbuf_tensor("x1T", [128, 16, T], BF16)
    x1T = x1T_cm.__enter__()
    b_x1T = P.buf("x1T")

    def layer_norm(src, bsrc, stats, mv, bst, lng, b_lng, dst_bf=None, bdst=None):
        def st(e):
            for c in range(4):
                i = e.bn_stats(out=stats[:, c, :], in_=src[:, c * 512:(c + 1) * 512])
            return i
        P.op("dve", st, reads=[bsrc], writes=[bst])
        P.op("dve", lambda e: e.bn_aggr(out=mv[:, 0:2], in_=stats.rearrange("p c s -> p (c s)")), reads=[bst], writes=[bst])
        P.op("dve", lambda e: e.tensor_scalar(out=mv[:, 2:3], in0=mv[:, 1:2], scalar1=1e-5, scalar2=None, op0=ALU.add), reads=[bst], writes=[bst])
        P.op("act", lambda e: e.activation(out=mv[:, 2:3], in_=mv[:, 2:3], func=AF.Ln), reads=[bst], writes=[bst])
        P.op("act", lambda e: e.activation(out=mv[:, 2:3], in_=mv[:, 2:3], func=AF.Exp, scale=-0.5), reads=[bst], writes=[bst])
        P.op("dve", lambda e: e.scalar_tensor_tensor(out=mv[:, 3:4], in0=mv[:, 0:1], scalar=-1.0, in1=mv[:, 2:3], op0=ALU.mult, op1=ALU.mult),
             reads=[bst], writes=[bst])
        P.op("act", lambda e: e.activation(out=src, in_=src, func=AF.Identity, bias=mv[:, 3:4], scale=mv[:, 2:3]), reads=[bst], writes=[bsrc])
        P.op("dve", lambda e: e.tensor_tensor(out=src, in0=src, in1=lng[:, 0, :], op=ALU.mult), reads=[b_lng], writes=[bsrc])
        P.op("dve", lambda e: e.tensor_tensor(out=src, in0=src, in1=lng[:, 1, :], op=ALU.add), reads=[b_lng], writes=[bsrc])
        if dst_bf is not None:
            P.op("pool", lambda e: e.tensor_copy(out=dst_bf, in_=src), reads=[bsrc], writes=[bdst])

    sc = Scope(nc)
    xr = sc.t("xr", [128, 8, D], F32)
    x1b = sc.t("x1b", [128, 2, D], BF16)
    lng = sc.t("lng1", [128, 2, D], F32)
    stats = sc.t("stats", [128, 2, 4, 6], F32)
    mv = sc.t("mv", [128, 2, 4], F32)
    with sc:
        b_xr = [P.buf(f"xr{t}") for t in range(8)]
        b_x1b = [P.buf("x1b0"), P.buf("x1b1")]
        b_st = [P.buf("st0"), P.buf("st1")]
        b_lng = P.buf("lng1")
        b_x1d = P.buf("x1d")
        P.dma("sp", lng[:], lnp[0:2, :].partition_broadcast(128), writes=[b_lng])
        for t in range(8):
            P.dma("sp", xr[:, t, :], xtok[t * 128:(t + 1) * 128, :], writes=[b_xr[t]])
        pr = Ring([0, 1, 2, 3])
        tring = Ring([4, 5, 6, 7])
        ln_pending = []

        def ln_tile(t):
            i2 = t % 2
            layer_norm(xr[:, t, :], b_xr[t], stats[:, i2], mv[:, i2], b_st[i2], lng, b_lng, x1b[:, i2, :], b_x1b[i2])
            P.dma("sp", x1d_ap[t * 128:(t + 1) * 128, :], xr[:, t, :], reads=[b_xr[t]], writes=[b_x1d])

            def trs(t=t, i2=i2):
                for c4 in range(4):
                    tp, btp = tring.next()
                    tpb = tp[:].bitcast(BF16)

                    def tf3(e, tpb=tpb, c4=c4):
                        for c in range(4):
                            k = c4 * 4 + c
                            i = e.transpose(out=tpb[:, c * 128:(c + 1) * 128], in_=x1b[:, i2, k * 128:(k + 1) * 128], identity=ident[:])
                        return i
                    P.op("pe", tf3, reads=[b_x1b[i2], b_const], writes=[btp])
                    P.op("act", lambda e, tpb=tpb, c4=c4: e.copy(out=x1T[:, c4 * 4:(c4 + 1) * 4, t * 128:(t + 1) * 128],
                                                               in_=tpb[:, 0:512].rearrange("p (c q) -> p c q", q=128)),
                         reads=[btp], writes=[b_x1T])
            ln_pending.append(trs)

        for n in range(4):
            s = n % 2
            for t in range(8):
                ps, bps = pr.next()

                def mmo(e, ps=ps, s=s, t=t):
                    for k in range(16):
                        i = e.matmul(ps[:], lhsT=mT[:, k, t * 128:(t + 1) * 128], rhs=wo[:, s, k, :], start=(k == 0), stop=(k == 15))
                    return i
                P.op("pe", mmo, reads=[b_mT, b_wo[s]], writes=[bps])
                P.op("dve", lambda e, ps=ps, t=t, n=n: e.scalar_tensor_tensor(out=xr[:, t, n * 512:(n + 1) * 512], in0=xr[:, t, n * 512:(n + 1) * 512],
                                                                          scalar=ALPHA, in1=ps[:], op0=ALU.mult, op1=ALU.add),
                     reads=[bps], writes=[b_xr[t]])
                if n == 3:
                    while len(ln_pending) > 1:
                        ln_pending.pop(0)()
                    ln_tile(t)
            if n + 2 < 4:
                load_wo(n + 2)
        while ln_pending:
            ln_pending.pop(0)()
        if DEBUG:
            P.dma("sp", dbg[5, :, :], x1T[:].rearrange("p h t -> p (h t)"), reads=[b_x1T], writes=[b_dbg])
        P.emit()
    wo_cm.__exit__(None, None, None)
    mT_cm.__exit__(None, None, None)

    hT_cm = nc.sbuf_tensor("hT", [128, 44, T], BF16, side="right")
    hT = hT_cm.__enter__()
    w2_cm = nc.sbuf_tensor("w2", [128, 2, 22, 256], BF16, side="right")
    w2 = w2_cm.__enter__()
    b_hT = P.buf("hT")
    b_w2 = [P.buf("w2a"), P.buf("w2b")]

    def load_w2(i):
        n, kh = i // 2, i % 2
        P.dma("pool", w2[:, i % 2], w_f2[kh * 2816:(kh + 1) * 2816, n * 256:(n + 1) * 256].rearrange("(k p) c -> p k c", p=128),
              writes=[b_w2[i % 2]])

    sc = Scope(nc)
    w1 = sc.t("w1", [128, 2, 2, 16, 256], BF16)
    sgl = sc.t("sgl", [128, 2, 512], F32)
    with sc:
        b_w1 = [P.buf("w1a"), P.buf("w1b")]
        b_sgl = [P.buf("sgl0"), P.buf("sgl1")]

        def load_w1(g):
            s = g % 2
            c0 = g * 256
            P.dma("pool", w1[:, s, 0], w_f1[:, c0:c0 + 256].rearrange("(k p) c -> p k c", p=128), writes=[b_w1[s]])
            P.dma("pool", w1[:, s, 1], w_f1[:, FF + c0:FF + c0 + 256].rearrange("(k p) c -> p k c", p=128), writes=[b_w1[s]])

        load_w1(0)
        load_w1(1)
        pr = Ring([0, 1, 2, 3, 4, 5, 6, 7])
        it = 0
        for g in range(22):
            s = g % 2
            for f in range(2):
                hid = g * 2 + f
                for th in range(2):
                    tc = slice(th * 512, (th + 1) * 512)
                    i2 = it % 2
                    it += 1
                    pg = pr.next()
                    pu = pr.next()

                    def mmf1(e, pg=pg, pu=pu, s=s, f=f, tc=tc):
                        for a, pp in ((0, pg), (1, pu)):
                            for k in range(16):
                                i = e.matmul(pp[0][:], lhsT=w1[:, s, a, k, f * 128:(f + 1) * 128], rhs=x1T[:, k, tc], start=(k == 0), stop=(k == 15))
                        return i
                    P.op("pe", mmf1, reads=[b_w1[s], b_x1T], writes=[pg[1], pu[1]])
                    P.op("act", lambda e, pg=pg, i2=i2: e.activation(out=sgl[:, i2, :], in_=pg[0][:], func=AF.Silu), reads=[pg[1]], writes=[b_sgl[i2]])
                    P.op("dve", lambda e, pu=pu, i2=i2, hid=hid, tc=tc: e.tensor_tensor(out=hT[:, hid, tc], in0=sgl[:, i2, :], in1=pu[0][:], op=ALU.mult),
                         reads=[b_sgl[i2], pu[1]], writes=[b_hT])
            if g + 2 < 22:
                load_w1(g + 2)
            if g == 19:
                load_w2(0)
            if g == 20:
                load_w2(1)
        P.emit()
    x1T_cm.__exit__(None, None, None)

    sc = Scope(nc)
    rr = sc.t("rr", [128, 8, D], F32)
    lng = sc.t("lng2", [128, 2, D], F32)
    stats = sc.t("stats2", [128, 2, 4, 6], F32)
    mv = sc.t("mv2", [128, 2, 4], F32)
    with sc:
        b_rr = [P.buf(f"rr{t}") for t in range(8)]
        b_st = [P.buf("st20"), P.buf("st21")]
        b_lng = P.buf("lng2")
        b_y = P.buf("y")
        for t in range(8):
            P.dma("sp", rr[:, t, :], x1d_ap[t * 128:(t + 1) * 128, :], writes=[b_rr[t]])
        P.dma("sp", lng[:], lnp[2:4, :].partition_broadcast(128), writes=[b_lng])
        for n in range(8):
            for kh in range(2):
                i = n * 2 + kh
                s = i % 2
                for t in ((0, 2, 4, 6, 1, 3, 5, 7) if kh == 1 else range(8)):
                    b = (n % 2) * 4 + t // 2
                    c0 = (t % 2) * 256
                    ps = banks[b]

                    def mmf2(e, ps=ps, s=s, t=t, kh=kh, c0=c0):
                        for k in range(22):
                            i_ = e.matmul(ps[:, c0:c0 + 256], lhsT=hT[:, kh * 22 + k, t * 128:(t + 1) * 128], rhs=w2[:, s, k, :],
                                          start=(kh == 0 and k == 0 and t % 2 == 0), stop=True, skip_group_check=True)
                        return i_
                    P.op("pe", mmf2, reads=[b_hT, b_w2[s]], writes=[bankb[b]])
                    if kh == 1:
                        P.op("dve", lambda e, ps=ps, t=t, n=n, c0=c0: e.scalar_tensor_tensor(
                            out=rr[:, t, n * 256:(n + 1) * 256], in0=rr[:, t, n * 256:(n + 1) * 256], scalar=ALPHA, in1=ps[:, c0:c0 + 256],
                            op0=ALU.mult, op1=ALU.add), reads=[bankb[b]], writes=[b_rr[t]])
                        if n == 7:
                            i2 = t % 2
                            layer_norm(rr[:, t, :], b_rr[t], stats[:, i2], mv[:, i2], b_st[i2], lng, b_lng)
                            P.dma("sp", y[t * 128:(t + 1) * 128, :], rr[:, t, :], reads=[b_rr[t]], writes=[b_y])
                if i + 2 < 16:
                    load_w2(i + 2)
        if DEBUG:
            P.dma("sp", dbgx1, x1d_ap, writes=[b_dbg])
        P.emit()
    w2_cm.__exit__(None, None, None)
    hT_cm.__exit__(None, None, None)
    return nc


_NC_CACHE = {}


def _tables(j):
    pos = np.concatenate([np.arange(j * 512, (j + 1) * 512), np.arange((7 - j) * 512, (8 - j) * 512)]).astype(np.float32)

    def rope(d):
        half = d // 2
        inv = (np.float32(10000.0) ** (-np.arange(half, dtype=np.float32) * np.float32(2.0) / np.float32(d))).astype(np.float32)
        ang = pos[:, None] * inv[None, :]
        c, s = np.cos(ang).astype(np.float32), np.sin(ang).astype(np.float32)
        return np.concatenate([c, c, -s, s], axis=1).astype(np.float32)
    ropeA, ropeB = rope(128), rope(64)
    vbias = np.full((8, 20), -1e30, np.float32)
    v01 = np.zeros((8, 20), np.float32)
    own = np.zeros((8, 20), np.float32)
    for t in range(8):
        hq, r = t // 4, t % 4
        segkb = j if hq == 0 else 7 - j
        for i in range(16):
            rr, hf = i // 4, (i // 2) % 2
            kb = rr if hf == 0 else 7 - rr
            if kb < segkb:
                vbias[t, i] = 0.0
                v01[t, i] = 1.0
        for l in range(4):
            lh, ls = l // 2, l % 2
            if lh == hq and ls == 0 and r >= 2:
                vbias[t, 16 + l] = 0.0
                v01[t, 16 + l] = 1.0
            if lh == hq and ls == r // 2:
                own[t, 16 + l] = 1.0
    mtab = np.concatenate([vbias.reshape(-1), v01.reshape(-1), own.reshape(-1)])[None, :].repeat(128, 0).astype(np.float32)
    slotb = np.zeros((128, 8), np.float32)
    for s in range(3):
        slotb[:, s] = 0.0 if s < j else NEG
    for kb in (4, 5, 6):
        slotb[:, 3 + kb - 4] = 0.0 if kb < 7 - j else NEG
    return ropeA, ropeB, mtab, slotb


def kernel(x, w_in, lambda_qk, diff_subln_w, w_branch_a, w_branch_b, w_out,
           ln1_g, ln1_b, w_ffn_in, w_ffn_out, ln2_g, ln2_b):
    x = np.asarray(x, np.float32)
    if "nc" not in _NC_CACHE:
        _NC_CACHE["nc"] = build_nc()
    nc = _NC_CACHE["nc"]
    kk = np.arange(128)
    tri = np.where(kk[:, None] <= kk[None, :], 0.0, NEG).astype(np.float32)
    cst = np.concatenate([tri, np.eye(128, dtype=np.float32)], axis=1)
    esel = np.zeros((128, 2560), np.float32)
    for i in range(20):
        esel[i, i * 128:(i + 1) * 128] = 1.0
    lnp = np.stack([np.asarray(a, np.float32).reshape(-1) for a in (ln1_g, ln1_b, ln2_g, ln2_b)], 0)
    shared = {
        "w_in": np.ascontiguousarray(np.asarray(w_in, np.float32)[0]),
        "w_ba": np.ascontiguousarray(np.asarray(w_branch_a, np.float32)[0]),
        "w_bb": np.ascontiguousarray(np.asarray(w_branch_b, np.float32)[0]),
        "w_out": np.ascontiguousarray(np.asarray(w_out, np.float32)[0]),
        "w_f1": np.ascontiguousarray(np.asarray(w_ffn_in, np.float32)[0]),
        "w_f2": np.ascontiguousarray(np.asarray(w_ffn_out, np.float32)[0]),
        "lnp": np.ascontiguousarray(lnp),
        "lam": np.ascontiguousarray(np.asarray(lambda_qk, np.float32).reshape(1, 256)),
        "subln": np.ascontiguousarray(np.asarray(diff_subln_w, np.float32).reshape(1, 128)),
        "cst": cst, "esel": esel,
    }
    in_maps = []
    for r in range(NCORES):
        b, j = r // 4, r % 4
        xt = np.concatenate([x[b, j * 512:(j + 1) * 512], x[b, (7 - j) * 512:(8 - j) * 512]], axis=0)
        ropeA, ropeB, mtab, slotb = _tables(j)
        m = dict(shared)
        m.update({"xtok": np.ascontiguousarray(xt), "xT": np.ascontiguousarray(xt.T),
                  "ropeA": ropeA, "ropeB": ropeB, "mtab": mtab, "slotb": slotb})
        in_maps.append(m)
    res = run_bass_kernel_spmd(nc, in_maps, core_ids=list(range(NCORES)))
    out = np.empty((2, 4096, D), np.float32)
    for r in range(NCORES):
        b, j = r // 4, r % 4
        yr = np.asarray(res.results[r]["y"], np.float32)
        out[b, j * 512:(j + 1) * 512] = yr[0:512]
        out[b, (7 - j) * 512:(8 - j) * 512] = yr[512:1024]
    if DEBUG:
        kernel.last = res
    return out
```
